# Optimizing a Trainium2 kernel written in Bass

```python
import math
import jax, jax.numpy as jnp
from jax import lax
import numpy as np

D_MODEL = 2048
BATCH = 32
SEQ = 256
DEPTH = 2
DEC_BATCH = 8
DEC_SEQ = 1024
PAST_LEN = 256

GRID_W = 64
D_MIX = D_MODEL
N_MIXERS = 4
D_BRANCH = D_MIX // N_MIXERS
N_FFT_GROUPS = 4
FFT_GROUP = D_BRANCH // N_FFT_GROUPS
POOL_WINDOWS = (2, 4, 8, 16)
POOL_GROUP = D_BRANCH // len(POOL_WINDOWS)
N_ATT_HEADS = 4
DIFF_HEAD = D_BRANCH // (2 * N_ATT_HEADS)
QK_DIM = 2 * DIFF_HEAD
V_DIM = 2 * DIFF_HEAD
N_FREQ = DIFF_HEAD // 4
CONV_WIDTH = 31
ROPE_BASE = 10000.0
EPS = 1e-6
Q_BLOCK = 128
N_IN_CHUNKS = 11
D_IN_PROJ = N_IN_CHUNKS * D_BRANCH

kernel_name = 'hybrid_diffusion_parallel_mixer_step'


def rmsnorm(x, g):
    xf = x.astype(jnp.float32)
    y = xf * lax.rsqrt(jnp.mean(xf * xf, axis=-1, keepdims=True) + EPS)
    return (y * g.astype(jnp.float32)).astype(x.dtype)


def layernorm(x, g, b):
    xf = x.astype(jnp.float32)
    mu = jnp.mean(xf, axis=-1, keepdims=True)
    var = jnp.mean(jnp.square(xf - mu), axis=-1, keepdims=True)
    y = (xf - mu) * lax.rsqrt(var + EPS)
    return (y * g.astype(jnp.float32) + b.astype(jnp.float32)).astype(x.dtype)


def axial_rope_tables(rows):
    t_row = jnp.repeat(jnp.arange(rows), GRID_W).astype(jnp.float32)
    t_col = jnp.tile(jnp.arange(GRID_W), rows).astype(jnp.float32)
    inv = ROPE_BASE ** (-jnp.arange(N_FREQ, dtype=jnp.float32) / N_FREQ)
    ang = jnp.stack([t_row[:, None] * inv, t_col[:, None] * inv], axis=1)
    return jnp.cos(ang), jnp.sin(ang)


def apply_rope(x, cos, sin):
    xs = x.astype(jnp.float32).reshape(x.shape[:-1] + (2, 2, N_FREQ))
    x1, x2 = xs[..., 0, :], xs[..., 1, :]
    cs, sn = cos[:, None], sin[:, None]
    y = jnp.stack([x1 * cs - x2 * sn, x2 * cs + x1 * sn], axis=-2)
    return y.reshape(x.shape).astype(x.dtype)


def fourier_mix(u, w_f):
    b, s, _ = u.shape
    uf = u.astype(jnp.float32).reshape(b, s, N_FFT_GROUPS, FFT_GROUP)
    mixed = jnp.fft.fft2(uf, axes=(1, 3), norm='ortho').real
    return mixed.reshape(b, s, D_BRANCH).astype(u.dtype) @ w_f


def pool_mix(u, w_pool, pool_scale):
    b, s, _ = u.shape
    uf = u.astype(jnp.float32)
    cs = jnp.pad(jnp.cumsum(uf, axis=1), ((0, 0), (1, 0), (0, 0)))
    t = jnp.arange(s)
    outs = []
    for gi, w in enumerate(POOL_WINDOWS):
        lo = jnp.clip(t - w // 2, 0, s)
        hi = jnp.clip(t - w // 2 + w, 0, s)
        sl = slice(gi * POOL_GROUP, (gi + 1) * POOL_GROUP)
        csg = cs[..., sl]
        mean = (csg[:, hi] - csg[:, lo]) / (hi - lo).astype(jnp.float32)[:, None]
        outs.append(mean - uf[..., sl])
    pooled = jnp.stack(outs, axis=2).astype(u.dtype)
    mixed = jnp.einsum('bsgc,gcd->bsgd', pooled, w_pool)
    return mixed.reshape(b, s, D_BRANCH) * pool_scale


def conv_module(a, g_glu, dw, dw_b, ln_g, ln_b, w_pw):
    u = a * jax.nn.sigmoid(g_glu)
    pad = CONV_WIDTH // 2
    y = lax.conv_general_dilated(u, dw[:, None, :].astype(u.dtype), window_strides=(1,),
                                 padding=[(pad, pad)], dimension_numbers=('NWC', 'WIO', 'NWC'),
                                 feature_group_count=D_BRANCH) + dw_b
    y = jax.nn.silu(layernorm(y, ln_g, ln_b))
    return y @ w_pw


def diff_attention(q, k, v, lam):
    b, h, sq = q.shape[:3]
    nblk = sq // Q_BLOCK
    qb = jnp.moveaxis(q.reshape(b, h, nblk, Q_BLOCK, 2, DIFF_HEAD), 2, 0)

    def block(qi):
        s = jnp.einsum('bhqmd,bhkmd->bhmqk', qi, k,
                       preferred_element_type=jnp.float32) * (DIFF_HEAD ** -0.5)
        p = jax.nn.softmax(s, axis=-1)
        att = p[:, :, 0] - lam * p[:, :, 1]
        return jnp.einsum('bhqk,bhkv->bhqv', att.astype(v.dtype), v)

    ob = lax.map(block, qb)
    return jnp.moveaxis(ob, 0, 2).reshape(b, h, sq, V_DIM)


def trunk_layer(x, cvec, l, p, rope=None, ctx_k=None, ctx_v=None):
    b, s, _ = x.shape
    shift, scale, gate = jnp.split(jax.nn.silu(cvec) @ p['w_mod'][l] + p['b_mod'][l], 3, axis=-1)
    h = rmsnorm(x, p['norm_g'][l]) * (1 + scale) + shift
    (f_x, f_g, p_x, p_g, q, k, v, a_g, c_a, c_b, c_g) = jnp.split(h @ p['w_in'][l], N_IN_CHUNKS, axis=-1)
    y_f = fourier_mix(f_x, p['w_fourier'][l]) * jax.nn.silu(f_g)
    y_p = pool_mix(p_x, p['w_pool'][l], p['pool_scale'][l]) * jax.nn.silu(p_g)
    y_c = conv_module(c_a, c_b, p['conv_dw'][l], p['conv_dw_b'][l], p['conv_ln_g'][l],
                      p['conv_ln_b'][l], p['w_conv_pw'][l]) * jax.nn.silu(c_g)
    q = q.reshape(b, s, N_ATT_HEADS, 2, DIFF_HEAD).transpose(0, 2, 1, 3, 4)
    k = k.reshape(b, s, N_ATT_HEADS, 2, DIFF_HEAD).transpose(0, 2, 1, 3, 4)
    v = v.reshape(b, s, N_ATT_HEADS, V_DIM).transpose(0, 2, 1, 3)
    lam_init = 0.8 - 0.6 * math.exp(-0.3 * l)
    lq1, lk1, lq2, lk2 = p['diff_lambda'][l].astype(jnp.float32)
    lam = jnp.exp(jnp.sum(lq1 * lk1)) - jnp.exp(jnp.sum(lq2 * lk2)) + lam_init
    if rope is None:
        o = diff_attention(q, k, v, lam)
    else:
        cos, sin = rope
        k_all = jnp.concatenate([ctx_k, apply_rope(k, cos, sin)], axis=2)
        v_all = jnp.concatenate([ctx_v, v], axis=2)
        o = diff_attention(apply_rope(q, cos, sin), k_all, v_all, lam)
    o = rmsnorm(o, p['subln_g'][l]) * (1 - lam_init)
    y_a = o.transpose(0, 2, 1, 3).reshape(b, s, D_BRANCH) * jax.nn.silu(a_g)
    y = jnp.concatenate([y_f, y_p, y_a, y_c], axis=-1) @ p['w_out'][l]
    return x + gate * y, k, v


def setup_inputs(seed: int = 0) -> dict:
    key = jax.random.key(seed)
    ks = jax.random.split(key, 24)
    f32 = jnp.float32

    def nrm(k, shape, scale=1.0):
        return jax.random.normal(k, shape, f32) * scale

    return {
        'x_prompt': nrm(ks[0], (BATCH, SEQ, D_MODEL)),
        'x_sample': nrm(ks[1], (DEC_BATCH, DEC_SEQ, D_MODEL)),
        'cache_k': nrm(ks[2], (DEC_BATCH, DEPTH, N_ATT_HEADS, PAST_LEN, QK_DIM)),
        'cache_v': nrm(ks[3], (DEC_BATCH, DEPTH, N_ATT_HEADS, PAST_LEN, V_DIM)),
        'c': nrm(ks[4], (DEC_BATCH, D_MODEL)),
        'c_ctx': nrm(ks[5], (D_MODEL,)),
        'norm_g': 1.0 + nrm(ks[6], (DEPTH, D_MODEL), 0.02),
        'w_mod': nrm(ks[7], (DEPTH, D_MODEL, 3 * D_MODEL), 0.3 * D_MODEL ** -0.5),
        'b_mod': nrm(ks[8], (DEPTH, 3 * D_MODEL), 0.02),
        'w_in': nrm(ks[9], (DEPTH, D_MODEL, D_IN_PROJ), D_MODEL ** -0.5),
        'w_fourier': nrm(ks[10], (DEPTH, D_BRANCH, D_BRANCH), D_BRANCH ** -0.5),
        'w_pool': nrm(ks[11], (DEPTH, len(POOL_WINDOWS), POOL_GROUP, POOL_GROUP), POOL_GROUP ** -0.5),
        'pool_scale': 1.0 + nrm(ks[12], (DEPTH, D_BRANCH), 0.02),
        'diff_lambda': nrm(ks[13], (DEPTH, 4, DIFF_HEAD), 0.1),
        'subln_g': 1.0 + nrm(ks[14], (DEPTH, V_DIM), 0.02),
        'conv_dw': nrm(ks[15], (DEPTH, CONV_WIDTH, D_BRANCH), CONV_WIDTH ** -0.5),
        'conv_dw_b': nrm(ks[16], (DEPTH, D_BRANCH), 0.02),
        'conv_ln_g': 1.0 + nrm(ks[17], (DEPTH, D_BRANCH), 0.02),
        'conv_ln_b': nrm(ks[18], (DEPTH, D_BRANCH), 0.02),
        'w_conv_pw': nrm(ks[19], (DEPTH, D_BRANCH, D_BRANCH), D_BRANCH ** -0.5),
        'w_out': nrm(ks[20], (DEPTH, D_MIX, D_MODEL), D_MIX ** -0.5),
        'final_g': 1.0 + nrm(ks[21], (D_MODEL,), 0.02),
    }


def reference(x_prompt, x_sample, cache_k, cache_v, c, c_ctx, norm_g, w_mod, b_mod, w_in,
              w_fourier, w_pool, pool_scale, diff_lambda, subln_g, conv_dw, conv_dw_b,
              conv_ln_g, conv_ln_b, w_conv_pw, w_out, final_g):
    p = {'norm_g': norm_g, 'w_mod': w_mod, 'b_mod': b_mod, 'w_in': w_in, 'w_fourier': w_fourier,
         'w_pool': w_pool, 'pool_scale': pool_scale, 'diff_lambda': diff_lambda,
         'subln_g': subln_g, 'conv_dw': conv_dw, 'conv_dw_b': conv_dw_b, 'conv_ln_g': conv_ln_g,
         'conv_ln_b': conv_ln_b, 'w_conv_pw': w_conv_pw, 'w_out': w_out}

    x = x_prompt
    ctx_keys, ctx_vals = [], []
    for l in range(DEPTH):
        x, k, v = trunk_layer(x, c_ctx, l, p)
        ctx_keys.append(k.reshape(k.shape[:3] + (QK_DIM,)))
        ctx_vals.append(v)
    y_prompt = rmsnorm(x, final_g)
    new_cache_k = jnp.stack(ctx_keys, axis=1)
    new_cache_v = jnp.stack(ctx_vals, axis=1)

    rows = x_sample.shape[1] // GRID_W
    rope = axial_rope_tables(rows)
    x = x_sample
    for l in range(DEPTH):
        ck = cache_k[:, l]
        ctx_k = ck.reshape(ck.shape[:3] + (2, DIFF_HEAD))
        x, _, _ = trunk_layer(x, c[:, None, :], l, p, rope, ctx_k, cache_v[:, l])
    y_sample = rmsnorm(x, final_g)
    return (y_prompt, y_sample, new_cache_k, new_cache_v)
```

```python
import math
from contextlib import ExitStack

import ml_dtypes
import numpy as np

import concourse.bass as bass
import concourse.mybir as mybir
from concourse.bass_utils import run_bass_kernel_spmd

F32 = mybir.dt.float32
BF16 = mybir.dt.bfloat16
AF = mybir.ActivationFunctionType
ALU = mybir.AluOpType
AX = mybir.AxisListType
NPBF = ml_dtypes.bfloat16

ENGS = ("pe", "act", "dve", "pool", "sp")
EPS = 1e-6
POOL_WINDOWS = (2, 4, 8, 16)
LAM_INIT = [0.8 - 0.6 * math.exp(-0.3 * l) for l in range(2)]
KB = 1024


class Op:
    __slots__ = ("eng", "fn", "reads", "writes", "dma", "waits", "signal", "eidx", "tick", "semval",
                 "prev_same_sem", "seq")

    def __init__(self, eng, fn, reads, writes, dma):
        self.eng = eng
        self.fn = fn
        self.reads = tuple(reads)
        self.writes = tuple(writes)
        self.dma = dma
        self.waits = []
        self.signal = False
        self.eidx = 0
        self.tick = 0
        self.semval = 0
        self.prev_same_sem = None


class Prog:
    def __init__(self, nc, es, dry=False):
        self.nc = nc
        self.es = es
        self.dry = dry
        self.ops = []
        self.bufacc = {}
        self.alias = {}
        self.live = []
        self.out_sems = set()
        self.last_dma_on_sem = {}

    def op(self, eng, fn, reads=(), writes=(), dma=None, is_output=False):
        if self.dry:
            return None
        o = Op(eng, fn, reads, writes, dma)
        o.seq = len(self.ops)
        self.ops.append(o)
        for k in o.reads + o.writes:
            d = self.bufacc.setdefault(k[0], {})
            if dma:
                d.setdefault("dma", []).append(o)
            else:
                d[eng] = o
        if dma:
            if not dma.startswith("G:"):
                o.prev_same_sem = self.last_dma_on_sem.get(dma)
                self.last_dma_on_sem[dma] = o
            if is_output:
                self.out_sems.add(dma)
        return o

    def arena_alloc(self, name, off, size):
        if self.dry:
            return
        end = off + size
        deps = []
        keep = []
        for (n, a, b) in self.live:
            if a < end and off < b:
                acc = self.bufacc.get(n, {})
                for e, o in acc.items():
                    if e == "dma":
                        deps.extend(o)
                    else:
                        deps.append(o)
                deps.extend(self.alias.get(n, ()))
            else:
                keep.append((n, a, b))
        keep.append((name, off, end))
        self.live = keep
        assert name not in self.bufacc, name
        best = {}
        dmas = []
        for o in deps:
            if o.dma:
                dmas.append(o)
            elif o.eng not in best or best[o.eng].seq < o.seq:
                best[o.eng] = o
        self.alias[name] = list(best.values()) + dmas

    def finalize(self):
        eidx = {e: 0 for e in ENGS}
        semcnt = {}
        for o in self.ops:
            eidx[o.eng] += 1
            o.eidx = eidx[o.eng]
            if o.dma:
                semcnt[o.dma] = semcnt.get(o.dma, 0) + 16
                o.semval = semcnt[o.dma]
        self.semtotal = semcnt
        last_w = {}
        readers = {}
        known = {e: {} for e in ENGS}
        for o in self.ops:
            deps = []
            for k in o.reads:
                w = last_w.get(k)
                if w is not None:
                    deps.append(w)
            for k in o.writes:
                w = last_w.get(k)
                if w is not None:
                    deps.append(w)
                r = readers.get(k)
                if r:
                    for e, v in r.items():
                        if e == "dma":
                            deps.extend(v)
                        else:
                            deps.append(v)
            names = set(k[0] for k in o.reads + o.writes)
            for n in names:
                al = self.alias.get(n)
                if al:
                    deps.extend(al)
            if o.prev_same_sem is not None:
                deps.append(o.prev_same_sem)
            kn = known[o.eng]
            engw = {}
            dmaw = {}
            for d in deps:
                if d is o:
                    continue
                if d.dma:
                    if o.dma == d.dma and d.dma.startswith("G:"):
                        continue
                    val = semcnt[d.dma] if d.dma.startswith("G:") else d.semval
                    key = "D:" + d.dma
                    if kn.get(key, 0) >= val:
                        continue
                    kn[key] = val
                    dmaw[d.dma] = max(dmaw.get(d.dma, 0), val)
                else:
                    if d.eng == "pe" and o.eng == "pe":
                        continue
                    key = "E:" + d.eng
                    if kn.get(key, 0) >= d.eidx:
                        continue
                    kn[key] = d.eidx
                    if d.eng not in engw or engw[d.eng].eidx < d.eidx:
                        engw[d.eng] = d
            for d in engw.values():
                d.signal = True
            o.waits = [("eng", d) for d in engw.values()] + [("dma", s, v) for s, v in dmaw.items()]
            for k in o.reads:
                r = readers.setdefault(k, {})
                if o.dma:
                    r.setdefault("dma", []).append(o)
                else:
                    r[o.eng] = o
            for k in o.writes:
                last_w[k] = o
                readers[k] = {}
        tick = {e: 0 for e in ENGS}
        for o in self.ops:
            if o.dma is None and o.signal:
                tick[o.eng] += 1
                o.tick = tick[o.eng]
        self.nticks = tick

    def emit(self):
        nc = self.nc
        handles = {"pe": nc.tensor, "act": nc.scalar, "dve": nc.vector, "pool": nc.gpsimd, "sp": nc.sync}
        esem = {e: self.es.enter_context(nc.semaphore("sem_" + e)) for e in ENGS}
        dsem = {}

        def getd(name):
            if name not in dsem:
                dsem[name] = self.es.enter_context(nc.semaphore("d_" + name.replace(":", "_")))
            return dsem[name]

        nwait = 0
        for o in self.ops:
            E = handles[o.eng]
            for w in o.waits:
                nwait += 1
                if w[0] == "eng":
                    d = w[1]
                    E.wait_ge(esem[d.eng], d.tick)
                else:
                    E.wait_ge(getd(w[1]), w[2])
            ins = o.fn(E)
            if o.dma:
                ins.then_inc(getd(o.dma), 16)
            elif o.signal:
                ins.then_inc(esem[o.eng], 1)
        for s in sorted(self.semtotal):
            nc.sync.wait_ge(getd(s), self.semtotal[s])
        for e in ENGS:
            if e != "sp" and self.nticks[e] > 0:
                nc.sync.wait_ge(esem[e], self.nticks[e])
        self.stats = dict(nops=len(self.ops), nwait=nwait, ticks=self.nticks, nsem=len(dsem) + len(esem))


def make_consts():
    c = {}

    def dft(S):
        s = np.arange(S, dtype=np.int64)
        ang = 2.0 * np.pi * ((np.outer(s, s) % S).astype(np.float64) / S)
        nrm = 1.0 / math.sqrt(S * 128.0)
        return np.stack([np.cos(ang) * nrm, np.sin(ang) * nrm]).astype(NPBF)

    c["c_dft256"] = dft(256)
    c["c_dft1024"] = dft(1024)
    s = np.arange(128, dtype=np.int64)
    ang = 2.0 * np.pi * ((np.outer(s, s) % 128).astype(np.float64) / 128)
    c["c_ccsc"] = np.stack([np.cos(ang), -np.sin(ang)]).astype(NPBF)
    S = 1024
    band = np.zeros((4, 5, 128, 128), np.float64)
    for g, w in enumerate(POOL_WINDOWS):
        B = np.zeros((S, S), np.float64)
        for t in range(S):
            lo = min(max(t - w // 2, 0), S)
            hi = min(max(t - w // 2 + w, 0), S)
            B[lo:hi, t] = 1.0 / (hi - lo)
            B[t, t] -= 1.0
        band[g, 0] = B[0:128, 0:128]
        band[g, 1] = B[128:256, 128:256]
        band[g, 2] = B[896:1024, 896:1024]
        band[g, 3] = B[0:128, 128:256]
        band[g, 4] = B[128:256, 0:128]
    c["c_band"] = np.ascontiguousarray(band.transpose(2, 0, 1, 3).reshape(128, 4 * 5 * 128)).astype(NPBF)
    c["c_ident"] = np.eye(128).astype(NPBF)
    c["c_identf"] = np.eye(128).astype(np.float32)
    c["c_onesm"] = np.full((128, 128), 1.0 / 512.0).astype(NPBF)
    c["c_mhalf"] = np.full((128, 8), -0.5, np.float32)
    inv = (np.float32(10000.0) ** (-np.arange(16, dtype=np.float32) / np.float32(16))).astype(np.float32)
    tok = np.arange(1024)
    pos = [np.floor_divide(tok, 64).astype(np.float32), np.mod(tok, 64).astype(np.float32)]
    rope = np.zeros((128, 2, 1024), np.float32)
    rperm = np.zeros((128, 128), np.float32)
    for p in range(128):
        a = (p % 64) // 32
        hh = (p % 32) // 16
        f = p % 16
        angp = (pos[a] * inv[f]).astype(np.float32)
        rope[p, 0] = np.cos(angp.astype(np.float64))
        rope[p, 1] = np.sin(angp.astype(np.float64)) * (-1.0 if hh == 0 else 1.0)
        partner = p + 16 if hh == 0 else p - 16
        rperm[partner, p] = 1.0
    c["c_rope"] = rope
    c["c_rperm"] = rperm
    return c


CONST_SPECS = {
    "c_dft256": ([2, 256, 256], BF16), "c_dft1024": ([2, 1024, 1024], BF16), "c_ccsc": ([2, 128, 128], BF16),
    "c_band": ([128, 2560], BF16), "c_ident": ([128, 128], BF16), "c_identf": ([128, 128], F32),
    "c_onesm": ([128, 128], BF16), "c_mhalf": ([128, 8], F32), "c_rope": ([128, 2, 1024], F32),
    "c_rperm": ([128, 128], F32),
}
IN_SPECS = {
    "xp": [1024, 2048], "xs": [1024, 2048], "ck": [2, 4, 256, 128], "cv": [2, 4, 256, 128], "cvec": [2, 2048],
    "norm_g": [2, 2048], "w_mod": [2, 2048, 6144], "b_mod": [2, 6144], "w_in": [2, 2048, 5632],
    "w_fourier": [2, 512, 512], "w_pool": [2, 4, 128, 128], "pool_scale": [2, 512], "diff_lambda": [2, 4, 64],
    "subln_g": [2, 128], "conv_dw": [2, 31, 512], "conv_dw_b": [2, 512], "conv_ln_g": [2, 512],
    "conv_ln_b": [2, 512], "w_conv_pw": [2, 512, 512], "w_out": [2, 2048, 2048], "final_g": [1, 2048],
}
OUT_SPECS = {"yp": [1024, 2048], "ys": [1024, 2048], "nk": [4, 2, 4, 256, 128], "nv": [4, 2, 4, 256, 128]}
ARENA_BYTES = 53 * KB


class Ring:
    def __init__(self, P, RING, plan=None):
        self.P = P
        self.RING = RING
        self.plan = plan
        self.rec = []
        self.issued = 0
        self.consumed = 0
        self.posted = 0
        self.handlers = {}

    def view(self, i, n):
        return self.RING[:, i % 3, :].rearrange("p (k n) -> p k n", n=n), ("ring", i % 3)

    def _issue(self, i):
        slot = i % 3
        src, k, n, post = self.plan[i]
        dst = self.RING[:, slot, :].rearrange("p (k n) -> p k n", n=n)
        self.P.op("pool", lambda e: e.dma_start(out=dst, in_=src), writes=[("ring", slot)], dma="ring%d" % slot)

    def _post(self, upto):
        while self.posted <= min(upto, len(self.plan) - 1):
            i = self.posted
            src, k, n, post = self.plan[i]
            if post is not None:
                v, rk = self.view(i, n)
                self.handlers[post[0]](v, rk, *post[1:])
            self.posted += 1

    def next(self, src, k, n, post=None):
        assert k * n == 4096
        i = self.consumed
        self.consumed += 1
        if self.plan is None:
            self.rec.append((src, k, n, post))
            return self.RING[:, 0, :].rearrange("p (k n) -> p k n", n=n), ("ring", 0)
        while self.issued <= min(i + 2, len(self.plan) - 1):
            self._issue(self.issued)
            self.issued += 1
        self._post(min(i + 1, self.issued - 1))
        return self.view(i, n)


class _Stop(Exception):
    pass


STOP = None


def stage(name):
    if STOP is not None and name == STOP:
        raise _Stop()


def build_all(nc, es, T, P, ring):
    try:
        _build_all(nc, es, T, P, ring)
    except _Stop:
        pass


def _build_all(nc, es, T, P, ring):
    X, HT, CAT, GBC, ARENA, PS, WTMP = T["X"], T["HT"], T["CAT"], T["GBC"], T["ARENA"], T["PS"], T["WTMP"]
    IDENT, IDENTF, RPERM, ONESM, CCSC, BAND, DFT256, MHALF = (T[k] for k in
                                                             ("IDENT", "IDENTF", "RPERM", "ONESM", "CCSC", "BAND", "DFT256", "MHALF"))
    SCT, CVT, BMT, GT_, MODT, AT, TMPA = (T[k] for k in ("SCT", "CVT", "BMT", "GT_", "MODT", "AT", "TMPA"))
    PST, DWBT, LNGT, LNBT, DWT, SGBC, NLAM = (T[k] for k in ("PST", "DWBT", "LNGT", "LNBT", "DWT", "SGBC", "NLAM"))
    SSQ, RSTD, TMP8 = T["SSQ"], T["RSTD"], T["TMP8"]
    D = T["dram"]
    C = ("c", 0)
    uid = [0]

    class PsA:
        def __init__(self):
            self.pool = list(range(8))
            self.n = 0

        def set_pool(self, banks):
            self.pool = list(banks)
            self.n = 0

        def get(self):
            b = self.pool[self.n % len(self.pool)]
            self.n += 1
            return b

    psa = PsA()

    def PSb(b):
        return PS[:, b * 512:(b + 1) * 512]

    def arena(base, off, nbytes, dt, shape=None):
        assert off % 4 == 0 and off + nbytes <= ARENA_BYTES, (base, off, nbytes)
        uid[0] += 1
        name = "%s#%d" % (base, uid[0])
        P.arena_alloc(name, off, nbytes)
        ap = ARENA[:, off // 2:(off + nbytes) // 2]
        if dt == F32:
            ap = ap.bitcast(F32)
        if shape is not None and len(shape) == 2:
            ap = ap.rearrange("p (a b) -> p a b", b=shape[1])
        elif shape is not None and len(shape) == 3:
            ap = ap.rearrange("p (a b c) -> p a b c", b=shape[1], c=shape[2])
        return ap, name

    def MM(out, lhsT, rhs, st, sp, r, w):
        P.op("pe", lambda e: e.matmul(out=out, lhsT=lhsT, rhs=rhs, start=st, stop=sp), r, w)

    def TRN(out, in_, r, w):
        P.op("pe", lambda e: e.transpose(out=out, in_=in_, identity=IDENT[:]), list(r) + [C], w)

    def ACTV(out, in_, func, r, w, **kw):
        P.op("act", lambda e: e.activation(out=out, in_=in_, func=func, **kw), r, w)

    def TT(eng, out, in0, in1, op, r, w):
        P.op(eng, lambda e: e.tensor_tensor(out=out, in0=in0, in1=in1, op=op), r, w)

    def TS(eng, out, in0, s1, s2, op0, op1, r, w):
        if s2 is None:
            P.op(eng, lambda e: e.tensor_scalar(out=out, in0=in0, scalar1=s1, scalar2=None, op0=op0), r, w)
        else:
            P.op(eng, lambda e: e.tensor_scalar(out=out, in0=in0, scalar1=s1, scalar2=s2, op0=op0, op1=op1), r, w)

    def STT(eng, out, in0, scalar, in1, op0, op1, r, w):
        P.op(eng, lambda e: e.scalar_tensor_tensor(out=out, in0=in0, scalar=scalar, in1=in1, op0=op0, op1=op1), r, w)

    def CP(eng, out, in_, r, w):
        if eng == "act":
            P.op(eng, lambda e: e.copy(out=out, in_=in_), r, w)
        else:
            P.op(eng, lambda e: e.tensor_copy(out=out, in_=in_), r, w)

    def DMA(eng, out, in_, r, w, sem, is_output=False, slow=False):
        if slow:
            P.op(eng, lambda e: e.dma_start(out=out, in_=in_, allow_slow_non_contiguous=True), r, w, dma=sem, is_output=is_output)
        else:
            P.op(eng, lambda e: e.dma_start(out=out, in_=in_), r, w, dma=sem, is_output=is_output)

    def MEMSET(eng, ap, val, w):
        P.op(eng, lambda e: e.memset(ap, val), [], w)

    def bc_last(ap, shape):
        return ap.broadcast_to(list(shape))

    G = "G:const"
    import os
    SKIP = os.environ.get("KSKIP", "").split(",")
    def CD(tag, out, in_, slow=False):
        if tag in SKIP:
            return
        DMA("sp", out, in_, [], [C], G, slow=slow)

    CD("id0", IDENT[:], D["c_ident"])
    CD("id1", IDENTF[:], D["c_identf"])
    CD("id2", RPERM[:], D["c_rperm"])
    CD("id3", ONESM[:], D["c_onesm"])
    CD("id4", MHALF[:], D["c_mhalf"])
    CD("ccsc", CCSC[:], D["c_ccsc"].rearrange("t p n -> p t n"))
    CD("band", BAND[:], D["c_band"])
    for t_ in range(2):
        CD("dft", DFT256[:, t_, :, :], D["c_dft256"][t_].rearrange("(k p) n -> p k n", p=128))
    CD("cvt", CVT[:], D["cvec"].rearrange("v (k p) -> p v k", p=128), slow=True)
    CD("bmt", BMT[:], D["b_mod"].rearrange("l (j p) -> p l j", p=128), slow=True)
    CD("gt", GT_[:], D["norm_g"].rearrange("l (k p) -> p l k", p=128), slow=True)
    for sbt, nm in ((PST, "pool_scale"), (DWBT, "conv_dw_b"), (LNGT, "conv_ln_g"), (LNBT, "conv_ln_b")):
        CD("vec4", sbt[:], D[nm].rearrange("l (j p) -> p l j", p=128), slow=True)
    for l in range(2):
        for j in range(4):
            CD("dwt", DWT[:, l, j, :], D["conv_dw"][l, :, j * 128:(j + 1) * 128].rearrange("t p -> p t"), slow=True)
        CD("sgbc", SGBC[:, l, :], D["subln_g"][l:l + 1, :].partition_broadcast(128))

    stage("p1")
    ACTV(SCT[:].rearrange("p k v -> p v k"), CVT[:], AF.Silu, [C], [("SCT", 0)])

    stage("p2")
    pending_mod = []

    def mod_units(l, bank):
        MODP = PSb(bank)[:, 0:96]
        fs = []

        def one(u):
            unit, rk = ring.next(D["w_mod"][l][:, u * 256:(u + 1) * 256].rearrange("(k p) n -> p k n", p=128), 16, 256)
            for cg in range(2):
                jc = u * 2 + cg
                for kc in range(16):
                    MM(MODP[:, jc * 2:jc * 2 + 2], unit[:, kc, cg * 128:(cg + 1) * 128], SCT[:, kc, :], kc == 0, kc == 15,
                       [rk, ("SCT", 0)], [("ps", bank)])
            if u == 23:
                TT("dve", MODT[:, l, :, :], MODP.rearrange("p (j v) -> p j v", v=2),
                   BMT[:, l, :].unsqueeze(2).broadcast_to([128, 48, 2]), ALU.add, [("ps", bank), C], [("MODT", l)])
                TS("dve", TMPA[:, l, :, :], MODT[:, l, 16:32, :], 1.0, None, ALU.add, None, [("MODT", l)], [("TMPA", l)])
                TT("dve", AT[:, l, :, :], TMPA[:, l, :, :], GT_[:, l, :].unsqueeze(2).broadcast_to([128, 16, 2]), ALU.mult,
                   [("TMPA", l), C], [("AT", l)])

        for u in range(24):
            fs.append(lambda u=u: one(u))
        return fs

    def interleave(k=1):
        for _ in range(k):
            if pending_mod:
                pending_mod.pop(0)()

    def flush_mod():
        while pending_mod:
            pending_mod.pop(0)()
        psa.set_pool(range(8))

    for f_ in mod_units(0, 0):
        f_()
    pending_mod.extend(mod_units(1, 7))
    psa.set_pool(range(1, 7))
    stage("p4")
    EL, SL, NL0 = T["EL"], T["SL"], T["NL0"]
    DL, DLn = arena("DL", 0, 2 * KB, F32)
    PR, PRn = arena("PR", 2 * KB, 1 * KB, F32, (4, 64))
    for l in range(2):
        DMA("sp", DL[:, l * 256:(l + 1) * 256],
            D["diff_lambda"].rearrange("l a d -> l (a d)")[l:l + 1, :].partition_broadcast(128), [], [(DLn, l)], "DL")
    DLv = DL.rearrange("p (a w d) -> p a w d", w=2, d=64)
    TT("dve", PR, DLv[:, :, 0, :], DLv[:, :, 1, :], ALU.mult, [(DLn, 0), (DLn, 1)], [(PRn, 0)])
    P.op("dve", lambda e: e.tensor_reduce(out=SL[:], in_=PR, axis=AX.X, op=ALU.add), [(PRn, 0)], [("SL", 0)])
    ACTV(EL[:], SL[:], AF.Exp, [("SL", 0)], [("EL", 0)])
    ELv = EL[:].rearrange("p (l a) -> p l a", a=2)
    TT("dve", NL0[:], ELv[:, :, 1], ELv[:, :, 0], ALU.subtract, [("EL", 0)], [("NL0", 0)])
    for l in range(2):
        TS("dve", NLAM[:, l:l + 1], NL0[:, l:l + 1], -LAM_INIT[l], None, ALU.add, None, [("NL0", 0)], [("NLAM", 0)])
        TS("dve", SGBC[:, l, :], SGBC[:, l, :], 1.0 - LAM_INIT[l], None, ALU.mult, None, [C], [("SGBC", 0)])

    stage("p5")
    XK = lambda t: [("X", t * 4 + i) for i in range(4)]

    def proj_tm(l, chunk, evac):
        for hf in range(2):
            c0 = chunk * 512 + hf * 256
            unit, rk = ring.next(D["w_in"][l][:, c0:c0 + 256].rearrange("(k p) n -> p k n", p=128), 16, 256)
            for tp in range(4):
                b = psa.get()
                for tt in range(2):
                    t = 2 * tp + tt
                    for kc in range(16):
                        MM(PSb(b)[:, tt * 256:(tt + 1) * 256], HT[:, kc, t * 128:(t + 1) * 128], unit[:, kc, :],
                           kc == 0, kc == 15, [rk, ("HT", kc)], [("ps", b)])
                evac(hf, tp, b)
            interleave(1)

    def proj_fm(l, chunk, evac, mid=None):
        for hf in range(2):
            if hf == 1 and mid is not None:
                mid()
            c0 = chunk * 512 + hf * 256
            unit, rk = ring.next(D["w_in"][l][:, c0:c0 + 256].rearrange("(k p) n -> p k n", p=128), 16, 256)
            for cg in range(2):
                j = hf * 2 + cg
                for tb in range(2):
                    b = psa.get()
                    for kc in range(16):
                        MM(PSb(b), unit[:, kc, cg * 128:(cg + 1) * 128], HT[:, kc, tb * 512:(tb + 1) * 512],
                           kc == 0, kc == 15, [rk, ("HT", kc)], [("ps", b)])
                    evac(j, tb, b)
            interleave(1)

    def fold(v, rk_, nh):
        TT("pool", v, v, GBC[:, nh * 1024:(nh + 1) * 1024].unsqueeze(1).broadcast_to([128, 4, 1024]), ALU.mult,
           [rk_, ("GBC", 0)], [rk_])

    ring.handlers["fold"] = fold

    def wout(l, bi):
        for nh in range(2):
            src = D["w_out"][l][bi * 512:(bi + 1) * 512, nh * 1024:(nh + 1) * 1024].rearrange("(k p) n -> p k n", p=128)
            unit, rk = ring.next(src, 4, 1024, post=("fold", nh))
            for t in range(8):
                for nb in range(2):
                    b = psa.get()
                    blk = nh * 2 + nb
                    c0 = blk * 512
                    for kc in range(4):
                        MM(PSb(b), CAT[:, kc, t * 128:(t + 1) * 128], unit[:, kc, nb * 512:(nb + 1) * 512],
                           kc == 0, kc == 3, [rk, ("CAT", kc)], [("ps", b)])
                    TT("dve", X[:, t, c0:c0 + 512], PSb(b), X[:, t, c0:c0 + 512], ALU.add,
                       [("ps", b), ("X", t * 4 + blk)], [("X", t * 4 + blk)])
            interleave(1)

    RK = [("RSTD", p) for p in range(4)]
    TK = [("TMP8", p) for p in range(4)]

    def rstd_all(nelem):
        TS("dve", TMP8[:], SSQ[:], 1.0 / nelem, EPS, ALU.mult, ALU.add, [("SSQ", t) for t in range(8)], TK)
        TT("pool", RSTD[:], TMP8[:], MHALF[:], ALU.pow, TK + [C], RK)

    def norm_phase(l, v):
        H, Hn = arena("H", 0, 32 * KB, BF16, (8, 2048))
        JUNK, Jn = arena("JUNK", 32 * KB, 4 * KB, BF16)
        def sq(t):
            ACTV(JUNK, X[:, t, :], AF.Square, XK(t), [(Jn, 0), ("SSQ", t)], accum_out=SSQ[:, t:t + 1])

        def rs(p):
            sl = slice(2 * p, 2 * p + 2)
            TS("pool", TMP8[:, sl], SSQ[:, sl], 1.0 / 2048.0, EPS, ALU.mult, ALU.add, [("SSQ", 2 * p), ("SSQ", 2 * p + 1)], [TK[p]])
            TT("pool", RSTD[:, sl], TMP8[:, sl], MHALF[:, 0:2], ALU.pow, [TK[p], C], [RK[p]])

        def idt(t):
            ACTV(H[:, t, :], X[:, t, :], AF.Identity, XK(t) + [RK[t // 2]], [(Hn, t)], scale=RSTD[:, t:t + 1])

        sq(0); sq(1); rs(0)
        sq(2); sq(3); rs(1)
        idt(0); idt(1)
        sq(4); sq(5); rs(2)
        idt(2); idt(3)
        sq(6); sq(7); rs(3)
        idt(4); idt(5); idt(6); idt(7)
        GTB, Gn = arena("GTB", 36 * KB, 8 * KB, F32, (16, 128))
        for kc in range(16):
            b = psa.get()
            pb = PSb(b).bitcast(BF16)
            for t in range(8):
                TRN(pb[:, t * 128:(t + 1) * 128], H[:, t, kc * 128:(kc + 1) * 128], [(Hn, t)], [("ps", b)])
            if kc % 2 == 0:
                TS("dve", HT[:, kc, :], pb, AT[:, l, kc, v:v + 1], MODT[:, l, kc, v:v + 1], ALU.mult, ALU.add,
                   [("ps", b), ("AT", l), ("MODT", l)], [("HT", kc)])
            else:
                ACTV(HT[:, kc, :], pb, AF.Identity, [("ps", b), ("AT", l), ("MODT", l)], [("HT", kc)],
                     scale=AT[:, l, kc, v:v + 1], bias=MODT[:, l, kc, v:v + 1])
        CP("dve", GTB, MODT[:, l, 32:48, v:v + 1].broadcast_to([128, 16, 128]), [("MODT", l)], [(Gn, 0)])
        for q4 in range(4):
            b = psa.get()
            for i in range(4):
                kc = q4 * 4 + i
                MM(PSb(b)[:, i * 128:(i + 1) * 128], GTB[:, kc, :], IDENTF[:], True, True, [(Gn, 0), C], [("ps", b)])
            CP("act", GBC[:, q4 * 512:(q4 + 1) * 512], PSb(b), [("ps", b)], [("GBC", 0)])

    def fourier(pi, l):
        NS, S = (4, 256) if pi == 0 else (1, 1024)
        FG, FGn = arena("FG", 0, 8 * KB, BF16, (4, 1024))
        FX, FXn = arena("FX", 8 * KB, 8 * KB, BF16, (8, 512))
        PT, PTn = arena("PT", 16 * KB, 16 * KB, BF16, (2, 4, 1024))
        WF, WFn = arena("WF", 32 * KB, 4 * KB, BF16, (4, 512))
        WCS, WCn = arena("WCS", 36 * KB, 8 * KB, BF16, (2, 4, 512))
        DMA("pool", WF, D["w_fourier"][l].rearrange("(g c) d -> c g d", c=128), [], [(WFn, 0)], "WF")

        def ev_fg(j, tb, b):
            ACTV(FG[:, j, tb * 512:(tb + 1) * 512], PSb(b), AF.Silu, [("ps", b)], [(FGn, j * 2 + tb)])

        proj_fm(l, 1, ev_fg)

        def ev_fx(hf, tp, b):
            CP("act", FX[:, 2 * tp:2 * tp + 2, hf * 256:(hf + 1) * 256], PSb(b).rearrange("p (t n) -> p t n", n=256),
               [("ps", b)], [(FXn, 2 * tp), (FXn, 2 * tp + 1)])

        proj_tm(l, 0, ev_fx)
        for cs in range(2):
            for g in range(4):
                b = psa.get()
                MM(PSb(b), CCSC[:, cs, :], WF[:, g, :], True, True, [C, (WFn, 0)], [("ps", b)])
                CP("dve", WCS[:, cs, g, :], PSb(b), [("ps", b)], [(WCn, cs * 4 + g)])
        if pi == 0:
            for n in range(4):
                for cs in range(2):
                    for gp in range(2):
                        b = psa.get()
                        for gg in range(2):
                            g = gp * 2 + gg
                            for kt in range(2):
                                MM(PSb(b)[:, gg * 256:(gg + 1) * 256], FX[:, n * 2 + kt, g * 128:(g + 1) * 128],
                                   DFT256[:, cs, kt, :], kt == 0, kt == 1, [(FXn, n * 2 + kt), C], [("ps", b)])
                        CP("act", PT[:, cs, gp * 2:gp * 2 + 2, n * 256:(n + 1) * 256],
                           PSb(b).rearrange("p (g n) -> p g n", n=256), [("ps", b)], [(PTn, cs * 2 + n // 2)])
        else:
            for cs in range(2):
                for half in range(2):
                    src = D["c_dft1024"][cs][:, half * 512:(half + 1) * 512].rearrange("(k p) n -> p k n", p=128)
                    unit, rk = ring.next(src, 8, 512)
                    for g in range(4):
                        b = psa.get()
                        for kt in range(8):
                            MM(PSb(b), FX[:, kt, g * 128:(g + 1) * 128], unit[:, kt, :], kt == 0, kt == 7,
                               [(FXn, kt), rk], [("ps", b)])
                        CP("act", PT[:, cs, g, half * 512:(half + 1) * 512], PSb(b), [("ps", b)], [(PTn, cs * 2 + half)])
        for j in range(4):
            for tb in range(2):
                b = psa.get()
                i = 0
                for cs in range(2):
                    for g in range(4):
                        MM(PSb(b), WCS[:, cs, g, j * 128:(j + 1) * 128], PT[:, cs, g, tb * 512:(tb + 1) * 512],
                           i == 0, i == 7, [(WCn, cs * 4 + g), (PTn, cs * 2 + tb)], [("ps", b)])
                        i += 1
                TT("dve", CAT[:, j, tb * 512:(tb + 1) * 512], PSb(b), FG[:, j, tb * 512:(tb + 1) * 512], ALU.mult,
                   [("ps", b), (FGn, j * 2 + tb)], [("CAT", j)])
        wout(l, 0)

    def poolmix(pi, l):
        NS, S = (4, 256) if pi == 0 else (1, 1024)
        NB = S // 128
        PG, PGn = arena("PG", 0, 8 * KB, BF16, (4, 1024))
        PX, PXn = arena("PX", 8 * KB, 8 * KB, BF16, (8, 512))
        PL, PLn = arena("PL", 16 * KB, 8 * KB, BF16, (4, 1024))
        WP, WPn = arena("WP", 24 * KB, 1 * KB, BF16, (4, 128))
        BANDv = BAND[:].rearrange("p (g t s) -> p g t s", t=5, s=128)
        DMA("pool", WP, D["w_pool"][l].rearrange("g c d -> c g d"), [], [(WPn, 0)], "WP")

        def ev_pg(j, tb, b):
            ACTV(PG[:, j, tb * 512:(tb + 1) * 512], PSb(b), AF.Silu, [("ps", b)], [(PGn, j * 2 + tb)])

        proj_fm(l, 3, ev_pg)

        def ev_px(hf, tp, b):
            CP("act", PX[:, 2 * tp:2 * tp + 2, hf * 256:(hf + 1) * 256], PSb(b).rearrange("p (t n) -> p t n", n=256),
               [("ps", b)], [(PXn, 2 * tp), (PXn, 2 * tp + 1)])

        proj_tm(l, 2, ev_px)
        for n in range(NS):
            for g in range(4):
                for grp in range((NB + 3) // 4):
                    b = psa.get()
                    nblk = min(4, NB - grp * 4)
                    for jl in range(nblk):
                        jt = grp * 4 + jl
                        srcs = []
                        if jt > 0:
                            srcs.append((jt - 1, 3))
                        srcs.append((jt, 0 if jt == 0 else (2 if jt == NB - 1 else 1)))
                        if jt < NB - 1:
                            srcs.append((jt + 1, 4))
                        for i, (kt, ty) in enumerate(srcs):
                            tile = n * NB + kt
                            MM(PSb(b)[:, jl * 128:(jl + 1) * 128], PX[:, tile, g * 128:(g + 1) * 128], BANDv[:, g, ty, :],
                               i == 0, i == len(srcs) - 1, [(PXn, tile), C], [("ps", b)])
                    t0 = n * S + grp * 512
                    CP("act", PL[:, g, t0:t0 + nblk * 128], PSb(b)[:, 0:nblk * 128], [("ps", b)], [(PLn, g * 2 + t0 // 512)])
        for g in range(4):
            for tb in range(2):
                b = psa.get()
                MM(PSb(b), WP[:, g, :], PL[:, g, tb * 512:(tb + 1) * 512], True, True, [(WPn, 0), (PLn, g * 2 + tb)], [("ps", b)])
                STT("dve", CAT[:, g, tb * 512:(tb + 1) * 512], PSb(b), PST[:, l, g:g + 1], PG[:, g, tb * 512:(tb + 1) * 512],
                    ALU.mult, ALU.mult, [("ps", b), C, (PGn, g * 2 + tb)], [("CAT", g)])
        wout(l, 1)

    def convmod(pi, l):
        NS, S = (4, 256) if pi == 0 else (1, 1024)
        SP = S + 30
        UPb, UPn = arena("UP", 0, 9 * KB, BF16)
        DGs = [arena("DG0", 9 * KB, 8 * KB, BF16), arena("DG1", 17 * KB, 8 * KB, BF16)]
        Y, Yn = arena("Y", 25 * KB, 16 * KB, F32, (4, 1024))
        SG, SGn = arena("SG", 41 * KB, 8 * KB, BF16, (4, 1024))
        WPW, WWn = arena("WPW", 49 * KB, 4 * KB, BF16, (4, 512))
        DMA("pool", WPW, D["w_conv_pw"][l].rearrange("(j c) d -> c j d", c=128), [], [(WWn, 0)], "WPW")
        UP = UPb[:, 0:4 * NS * SP].rearrange("p (j n s) -> p j n s", n=NS, s=SP)
        MEMSET("pool", UPb, 0.0, [(UPn, 0)])

        def ev_sg(j, tb, b):
            ACTV(SG[:, j, tb * 512:(tb + 1) * 512], PSb(b), AF.Sigmoid, [("ps", b)], [(SGn, j * 2 + tb)])

        proj_fm(l, 9, ev_sg)

        def ev_u(j, tb, b):
            if pi == 0:
                TT("dve", UP[:, j, 2 * tb:2 * tb + 2, 15:15 + 256], PSb(b).rearrange("p (n s) -> p n s", s=256),
                   SG[:, j, tb * 512:(tb + 1) * 512].rearrange("p (n s) -> p n s", s=256), ALU.mult,
                   [("ps", b), (SGn, j * 2 + tb), (UPn, 0)], [(UPn, 0)])
            else:
                TT("dve", UP[:, j, 0, 15 + tb * 512:15 + (tb + 1) * 512], PSb(b), SG[:, j, tb * 512:(tb + 1) * 512], ALU.mult,
                   [("ps", b), (SGn, j * 2 + tb), (UPn, 0)], [(UPn, 0)])

        proj_fm(l, 8, ev_u)
        NPE = 24
        for j in range(4):
            DG, DGn = DGs[j % 2]
            DGv = DG[:, 0:NPE * 128].rearrange("p (t c) -> p t c", c=128)
            TT("pool", DGv, IDENT[:].unsqueeze(1).broadcast_to([128, NPE, 128]),
               DWT[:, l, j, 0:NPE].unsqueeze(2).broadcast_to([128, NPE, 128]), ALU.mult, [C], [(DGn, 0)])

            def yblk(tb):
                if pi == 0:
                    return Y[:, j, tb * 512:(tb + 1) * 512].rearrange("p (n s) -> p n s", s=256)
                return Y[:, j, tb * 512:(tb + 1) * 512]

            def ushift(tb, tap):
                if pi == 0:
                    return UP[:, j, 2 * tb:2 * tb + 2, tap:tap + 256]
                return UP[:, j, 0, tb * 512 + tap:tb * 512 + tap + 512]

            for tap in range(NPE, 31):
                for tb in range(2):
                    yk = (Yn, j * 2 + tb)
                    if tap == NPE:
                        TS("dve", yblk(tb), ushift(tb, tap), DWT[:, l, j, tap:tap + 1], None, ALU.mult, None, [(UPn, 0), C], [yk])
                    else:
                        STT("dve", yblk(tb), ushift(tb, tap), DWT[:, l, j, tap:tap + 1], yblk(tb), ALU.mult, ALU.add,
                            [(UPn, 0), C, yk], [yk])
            for tb in range(2):
                b = psa.get()
                if pi == 0:
                    for nn in range(2):
                        n = 2 * tb + nn
                        for tap in range(NPE):
                            MM(PSb(b)[:, nn * 256:(nn + 1) * 256], DGv[:, tap, :], UP[:, j, n, tap:tap + 256],
                               tap == 0, tap == NPE - 1, [(DGn, 0), (UPn, 0)], [("ps", b)])
                    pv_ = PSb(b).rearrange("p (n s) -> p n s", s=256)
                else:
                    for tap in range(NPE):
                        MM(PSb(b), DGv[:, tap, :], UP[:, j, 0, tb * 512 + tap:tb * 512 + tap + 512],
                           tap == 0, tap == NPE - 1, [(DGn, 0), (UPn, 0)], [("ps", b)])
                    pv_ = PSb(b)
                STT("dve", yblk(tb), pv_, DWBT[:, l, j:j + 1], yblk(tb), ALU.add, ALU.add,
                    [("ps", b), C, (Yn, j * 2 + tb)], [(Yn, j * 2 + tb)])

        YB, YBn = arena("YB", 0, 4 * KB, BF16, (4, 512))
        YS, YSn = arena("YS", 4 * KB, 4 * KB, BF16, (4, 512))
        MEAN, MEn = arena("MEAN", 8 * KB, 2 * KB, F32)
        MSQ, MSn = arena("MSQ", 10 * KB, 2 * KB, F32)
        RS, RSn = arena("RS", 12 * KB, 2 * KB, F32)
        TB_, TBn = arena("TB", 14 * KB, 8 * KB, F32, (4, 512))
        ZB, ZBn = arena("ZB", 41 * KB, 8 * KB, BF16, (4, 1024))
        def ln(tb):
            ysl = Y[:, :, tb * 512:(tb + 1) * 512]
            YK = [(Yn, j_ * 2 + tb) for j_ in range(4)]
            ACTV(YB, ysl, AF.Identity, YK, [(YBn, 0)])
            ACTV(YS, ysl, AF.Square, YK, [(YSn, 0)])
            bm = psa.get()
            for j in range(4):
                MM(PSb(bm), ONESM[:], YB[:, j, :], j == 0, j == 3, [C, (YBn, 0)], [("ps", bm)])
            be = psa.get()
            for j in range(4):
                MM(PSb(be), ONESM[:], YS[:, j, :], j == 0, j == 3, [C, (YSn, 0)], [("ps", be)])
            ACTV(MEAN, PSb(bm), AF.Identity, [("ps", bm)], [(MEn, 0)])
            ACTV(MSQ, PSb(bm), AF.Square, [("ps", bm)], [(MSn, 0)])
            STT("dve", RS, PSb(be), EPS, MSQ, ALU.add, ALU.subtract, [("ps", be), (MSn, 0)], [(RSn, 0)])
            ACTV(RS, RS, AF.Ln, [(RSn, 0)], [(RSn, 0)])
            ACTV(RS, RS, AF.Exp, [(RSn, 0)], [(RSn, 0)], scale=-0.5)
            TT("dve", TB_, ysl, MEAN.unsqueeze(1).broadcast_to([128, 4, 512]), ALU.subtract, YK + [(MEn, 0)], [(TBn, 0)])
            TT("dve", TB_, TB_, RS.unsqueeze(1).broadcast_to([128, 4, 512]), ALU.mult, [(TBn, 0), (RSn, 0)], [(TBn, 0)])
            for j in range(4):
                ACTV(ZB[:, j, tb * 512:(tb + 1) * 512], TB_[:, j, :], AF.Silu, [(TBn, 0), C], [(ZBn, tb)],
                     scale=LNGT[:, l, j:j + 1], bias=LNBT[:, l, j:j + 1])

        ln(0)
        def ev_cg(j, tb, b):
            ACTV(CAT[:, j, tb * 512:(tb + 1) * 512], PSb(b), AF.Silu, [("ps", b)], [("CAT", j)])

        proj_fm(l, 10, ev_cg, mid=lambda: ln(1))

        for jo in range(4):
            for tb in range(2):
                b = psa.get()
                for j in range(4):
                    MM(PSb(b), WPW[:, j, jo * 128:(jo + 1) * 128], ZB[:, j, tb * 512:(tb + 1) * 512], j == 0, j == 3,
                       [(WWn, 0), (ZBn, tb)], [("ps", b)])
                TT("dve", CAT[:, jo, tb * 512:(tb + 1) * 512], PSb(b), CAT[:, jo, tb * 512:(tb + 1) * 512], ALU.mult,
                   [("ps", b), ("CAT", jo)], [("CAT", jo)])
        wout(l, 3)

    def attention(pi, l):
        NS, S = (4, 256) if pi == 0 else (1, 1024)
        NKT = 2 if pi == 0 else 10
        KOFF = 0 if pi == 0 else 256
        SK = S + KOFF
        NVT = 8 if pi == 0 else 10
        vx_sz = (8 * KB + 512) if pi == 0 else (10 * KB + 512)
        kt_sz = 8 * KB if pi == 0 else 10 * KB
        a_kt = vx_sz
        a_qz = a_kt + kt_sz
        o2 = a_qz + 16 * KB
        VX, VXn = arena("VX", 0, vx_sz, BF16)
        KT, KTn = arena("KT", a_kt, kt_sz, BF16)
        QZ, QZn = arena("QZ", a_qz, 16 * KB, BF16)
        QZv = QZ.rearrange("p (h q m s) -> p h q m s", q=4, m=2, s=256)
        VXv = VX[:, 0:NVT * 4 * 130].rearrange("p (t h d) -> p t h d", h=4, d=130)
        KTv = KT[:, 0:4 * NS * SK].rearrange("p (h s) -> p h s", s=NS * SK)
        MEMSET("pool", VX, 1.0, [(VXn, "all")])
        MEMSET("pool", QZ, 0.0, [(QZn, "all")])

        if pi == 0:
            ST, STn = arena("ST", o2, 8 * KB, F32, (4, 512))
        stc = [0]

        def stage_out(dst, hf, tp, b):
            i = stc[0] % 4
            stc[0] += 1
            CP("act", ST[:, i, :], PSb(b), [("ps", b)], [(STn, i)])
            for tt in range(2):
                t = 2 * tp + tt
                n, st_ = t // 2, t % 2
                dd = D[dst][n, l, 2 * hf:2 * hf + 2, st_ * 128:(st_ + 1) * 128, :].rearrange("h s d -> s h d")
                DMA("sp", dd, ST[:, i, tt * 256:(tt + 1) * 256].rearrange("p (h d) -> p h d", d=128), [(STn, i)], [],
                    "ST%d" % i, is_output=True)
            return i

        vt0 = 0 if pi == 0 else 2

        def ev_v(hf, tp, b):
            dstv = VXv[:, vt0 + 2 * tp:vt0 + 2 * tp + 2, 2 * hf:2 * hf + 2, 0:128]
            wk = [(VXn, vt0 + 2 * tp), (VXn, vt0 + 2 * tp + 1)]
            if pi == 0:
                i = stage_out("nv", hf, tp, b)
                CP("dve", dstv, ST[:, i, :].rearrange("p (t h d) -> p t h d", h=2, d=128), [(STn, i), (VXn, "all")], wk)
            else:
                CP("dve", dstv, PSb(b).rearrange("p (t h d) -> p t h d", h=2, d=128), [("ps", b), (VXn, "all")], wk)

        proj_tm(l, 6, ev_v)
        stage("a2%d%d" % (pi, l))
        if pi == 0:
            KRAW, KRn = arena("KRAW", o2 + 8 * KB, 4 * KB, F32, (2, 512))
        else:
            ROPE, ROn = arena("ROPE", o2, 8 * KB, F32, (2, 1024))
            RAW, RWn = arena("RAW", o2 + 8 * KB, 4 * KB, F32, (2, 512))
            CKB, CKn = arena("CKB", o2 + 12 * KB, 2 * KB, BF16, (4, 2, 128))
            DMA("sp", ROPE, D["c_rope"], [], [(ROn, 0)], "ROPE")
            for kt_ in range(2):
                DMA("pool", VXv[:, kt_, :, 0:128], D["cv"][l][:, kt_ * 128:(kt_ + 1) * 128, :].rearrange("h p d -> p h d"),
                    [(VXn, "all")], [(VXn, kt_)], "CVL")
                DMA("pool", CKB[:, :, kt_, :], D["ck"][l][:, kt_ * 128:(kt_ + 1) * 128, :].rearrange("h p d -> p h d"),
                    [], [(CKn, 0)], "CKL")
            b = psa.get()
            pb = PSb(b).bitcast(BF16)
            for h in range(4):
                for kt in range(2):
                    TRN(pb[:, (h * 2 + kt) * 128:(h * 2 + kt + 1) * 128], CKB[:, h, kt, :], [(CKn, 0)], [("ps", b)])
            CP("dve", KTv[:, :, 0:256], pb.rearrange("p (h s) -> p h s", s=256), [("ps", b)], [(KTn, "ctx")])
            T1, T1n = arena("T1", o2 + 12 * KB, 2 * KB, F32)
            T2, T2n = arena("T2", o2 + 14 * KB, 2 * KB, F32)
        stage("a3%d%d" % (pi, l))
        rc = [0]

        def roped(j, tb, b):
            i = rc[0] % 2
            rc[0] += 1
            CP("act", RAW[:, i, :], PSb(b), [("ps", b)], [(RWn, i)])
            b2 = psa.get()
            MM(PSb(b2), RPERM[:], RAW[:, i, :], True, True, [C, (RWn, i)], [("ps", b2)])
            TT("pool", T1, RAW[:, i, :], ROPE[:, 0, tb * 512:(tb + 1) * 512], ALU.mult, [(RWn, i), (ROn, 0)], [(T1n, 0)])
            TT("dve", T2, PSb(b2), ROPE[:, 1, tb * 512:(tb + 1) * 512], ALU.mult, [("ps", b2), (ROn, 0)], [(T2n, 0)])

        def ev_k(j, tb, b):
            dst = KTv[:, j, KOFF + tb * 512:KOFF + (tb + 1) * 512]
            if pi == 0:
                CP("act", dst, PSb(b), [("ps", b)], [(KTn, j * 2 + tb)])
                i = rc[0] % 2
                rc[0] += 1
                CP("act", KRAW[:, i, :], PSb(b), [("ps", b)], [(KRn, i)])
                b2 = psa.get()
                for q in range(4):
                    P.op("pe", lambda e, q=q: e.transpose(out=PSb(b2)[:, q * 128:(q + 1) * 128],
                                                          in_=KRAW[:, i, q * 128:(q + 1) * 128], identity=IDENTF[:]),
                         [(KRn, i), C], [("ps", b2)])
                si = stc[0] % 4
                stc[0] += 1
                CP("act", ST[:, si, :], PSb(b2), [("ps", b2)], [(STn, si)])
                for nn in range(2):
                    dd = D["nk"][2 * tb + nn, l, j, :, :].rearrange("(q p) d -> p q d", p=128)
                    DMA("sp", dd, ST[:, si, nn * 256:(nn + 1) * 256].rearrange("p (q d) -> p q d", d=128), [(STn, si)], [],
                        "ST%d" % si, is_output=True)
            else:
                roped(j, tb, b)
                TT("dve", dst, T1, T2, ALU.add, [(T1n, 0), (T2n, 0)], [(KTn, j * 2 + tb)])

        def ev_q(j, tb, b):
            for m in range(2):
                rows = slice(m * 64, (m + 1) * 64)
                dst = QZv[rows, j, 2 * tb:2 * tb + 2, m, :]
                if pi == 0:
                    CP("act", dst, PSb(b)[rows, :].rearrange("p (q s) -> p q s", s=256), [("ps", b), (QZn, "all")],
                       [(QZn, j * 2 + tb)])
                else:
                    if m == 0:
                        roped(j, tb, b)
                    TT("dve", dst, T1[rows, :].rearrange("p (q s) -> p q s", s=256),
                       T2[rows, :].rearrange("p (q s) -> p q s", s=256), ALU.add,
                       [(T1n, 0), (T2n, 0), (QZn, "all")], [(QZn, j * 2 + tb)])

        proj_fm(l, 5, ev_k)
        stage("a4%d%d" % (pi, l))
        proj_fm(l, 4, ev_q)

        AG, AGn = arena("AG", o2, 8 * KB, BF16, (8, 512))

        def ev_ag(hf, tp, b):
            agv = AG[:, 2 * tp:2 * tp + 2, hf * 256:(hf + 1) * 256]
            ACTV(agv, PSb(b).rearrange("p (t n) -> p t n", n=256), AF.Silu,
                 [("ps", b)], [(AGn, 2 * tp), (AGn, 2 * tp + 1)])
            for tt in range(2):
                a1 = AG[:, 2 * tp + tt, hf * 256:(hf + 1) * 256].rearrange("p (h d) -> p h d", d=128)
                TT("pool", a1, a1, SGBC[:, l, :].unsqueeze(1).broadcast_to([128, 2, 128]), ALU.mult,
                   [(AGn, 2 * tp + tt), ("SGBC", 0)], [(AGn, 2 * tp + tt)])

        proj_tm(l, 7, ev_ag)
        stage("core%d%d" % (pi, l))

        EB, EBn = arena("EB", o2 + 8 * KB, 3 * KB, BF16, (3, 512))
        fo = o2 + 11 * KB
        QN = 256
        NQT = 2
        NSET = 3
        sets = []
        for si in range(NSET):
            f0 = fo + si * 1664
            sets.append(dict(
                R=arena("R%d" % si, f0, 32, F32, (2, 2)), R1=arena("R1%d" % si, f0 + 32, 32, F32),
                SS=arena("SS%d" % si, f0 + 64, 32, F32), RS=arena("RSa%d" % si, f0 + 96, 32, F32),
                O0=arena("O0%d" % si, f0 + 128, 1 * KB, F32, (2, 128)),
                OA=arena("OA%d" % si, f0 + 128 + KB, 512, BF16, (2, 128))))
        groups = []
        for n in range(NS):
            for h in range(4):
                for qb in range(S // QN):
                    groups.append((n, h, qb))
        steps = []
        for gi in range(len(groups)):
            for kt in range(NKT):
                steps.append((gi, kt))
        DEPTH = 2

        def ginfo(gi):
            n, h, qb = groups[gi]
            par = gi % NSET
            ob = 2 + 2 * par
            Ov = PS[:, ob * 512:(ob + 2) * 512].rearrange("p (q c) -> p q c", c=512)
            return n, h, qb, par, ob, Ov, n * S + qb * QN

        def score(si):
            gi, kt = steps[si]
            n, h, qb, par, ob, Ov, q0 = ginfo(gi)
            bs = si % 2
            k0 = n * SK + kt * 128
            qbg = q0 // 256
            MM(PSb(bs), KTv[:, h, k0:k0 + 128], QZv[:, h, qbg, :, :], True, True,
               [(KTn, x) for x in ("ctx", h * 2, h * 2 + 1)] + [(QZn, "all"), (QZn, h * 2 + qbg // 2)], [("ps", bs)])
            ei = si % 3
            ACTV(EB[:, ei, :], PSb(bs), AF.Exp, [("ps", bs)], [(EBn, ei)], scale=0.125)

        def pv(si):
            gi, kt = steps[si]
            n, h, qb, par, ob, Ov, q0 = ginfo(gi)
            ei = si % 3
            vt = (n * 2 + kt) if pi == 0 else kt
            for m in range(2):
                for qt in range(NQT):
                    P.op("pe", lambda e, m=m, qt=qt: e.matmul(
                        out=Ov[:, qt, m * 130:(m + 1) * 130], lhsT=EB[:, ei, m * 256 + qt * 128:m * 256 + (qt + 1) * 128],
                        rhs=VXv[:, vt, h, :], start=(kt == 0 and m == 0), stop=(kt == NKT - 1), skip_group_check=True),
                        [(EBn, ei), (VXn, vt), (VXn, "all")], [("ps", ob + qt)])
            if kt == NKT - 1:
                finalize(gi)
                if gi >= 1:
                    finalize_b(gi - 1)
                if gi >= 2:
                    finalize_pe(gi - 2)
                if gi == len(groups) - 1:
                    finalize_b(gi)
                    finalize_pe(gi - 1)
                    finalize_pe(gi)

        def finalize(gi):
            n, h, qb, par, ob, Ov, q0 = ginfo(gi)
            st_ = sets[par]
            (R_, Rn), (R1, R1n), (SSa, SSn), (RSa, RSan) = st_["R"], st_["R1"], st_["SS"], st_["RS"]
            R_ = R_[:, 0:2, :]
            R1 = R1[:, 0:NQT]
            (O0, O0n), (OA, OAn) = st_["O0"], st_["OA"]
            OR = [("ps", ob + qt) for qt in range(NQT)]
            Ow = Ov[:, :, 0:260].rearrange("p q (m d) -> p q m d", d=130)
            P.op("dve", lambda e: e.reciprocal(out=R_, in_=Ow[:, :, :, 128]), OR, [(Rn, 0)])
            TS("dve", R1, R_[:, :, 1], NLAM[:, l:l + 1], None, ALU.mult, None, [(Rn, 0), ("NLAM", 0)], [(R1n, 0)])
            TT("dve", O0, Ow[:, :, 0, 0:128], R_[:, :, 0:1].broadcast_to([128, NQT, 128]), ALU.mult, OR + [(Rn, 0)], [(O0n, 0)])
            for qt in range(NQT):
                STT("dve", O0[:, qt, :], Ow[:, qt, 1, 0:128], R1[:, qt:qt + 1], O0[:, qt, :], ALU.mult, ALU.add,
                    OR + [(R1n, 0), (O0n, 0)], [(O0n, 0)])
            for qt in range(NQT):
                P.op("dve", lambda e, qt=qt: e.scalar_tensor_tensor(out=OA[:, qt, :], in0=O0[:, qt, :], scalar=1.0, in1=O0[:, qt, :],
                                                                    op0=ALU.mult, op1=ALU.mult, accum_out=SSa[:, qt:qt + 1]),
                     [(O0n, 0)], [(OAn, 0), (SSn, qt)])
            TS("pool", RSa[:, 0:NQT], SSa[:, 0:NQT], 1.0 / 128.0, EPS, ALU.mult, ALU.add, [(SSn, 0), (SSn, 1)], [(RSan, 0)])
            TT("pool", RSa[:, 0:NQT], RSa[:, 0:NQT], MHALF[:, 0:NQT], ALU.pow, [(RSan, 0), C], [(RSan, 0)])

        def finalize_b(gi):
            n, h, qb, par, ob, Ov, q0 = ginfo(gi)
            st_ = sets[par]
            (RSa, RSan), (O0, O0n), (OA, OAn) = st_["RS"], st_["O0"], st_["OA"]
            tl0 = q0 // 128
            for qt in range(NQT):
                STT("dve", OA[:, qt, :], O0[:, qt, :], RSa[:, qt:qt + 1], AG[:, tl0 + qt, h * 128:(h + 1) * 128], ALU.mult, ALU.mult,
                    [(O0n, 0), (RSan, 0), (AGn, tl0 + qt)], [(OAn, 0)])

        def finalize_pe(gi):
            n, h, qb, par, ob, Ov, q0 = ginfo(gi)
            (OA, OAn) = sets[par]["OA"]
            pbt = PSb(ob).bitcast(BF16)[:, 640:896]
            for qt in range(NQT):
                TRN(pbt[:, qt * 128:(qt + 1) * 128], OA[:, qt, :], [(OAn, 0)], [("ps", ob)])
            CP("act", CAT[:, h, q0:q0 + QN], pbt, [("ps", ob)], [("CAT", h)])

        for si in range(len(steps) + DEPTH):
            if si < len(steps):
                score(si)
            if si - DEPTH >= 0:
                pv(si - DEPTH)
        wout(l, 2)

    def final_norm(pi):
        yout = D["yp"] if pi == 0 else D["ys"]
        FGB, FGn = arena("FGB", 0, 8 * KB, F32)
        JUNK, Jn = arena("JUNK", 8 * KB, 4 * KB, BF16)
        YO, YOn = arena("YO", 12 * KB, 16 * KB, F32, (2, 2048))
        SSF, SSFn = arena("SSF", 28 * KB, 32, F32)
        TMF, TMFn = arena("TMF", 28 * KB + 32, 32, F32)
        RSF, RSFn = arena("RSF", 28 * KB + 64, 32, F32)
        DMA("sp", FGB, D["final_g"].partition_broadcast(128), [], [(FGn, 0)], "FGB")
        for hf in range(2):
            ts_ = range(4 * hf, 4 * hf + 4)
            for t in ts_:
                ACTV(JUNK, X[:, t, :], AF.Square, XK(t), [(Jn, 0), (SSFn, t)], accum_out=SSF[:, t:t + 1])
            sl = slice(4 * hf, 4 * hf + 4)
            TS("pool", TMF[:, sl], SSF[:, sl], 1.0 / 2048.0, EPS, ALU.mult, ALU.add, [(SSFn, t) for t in ts_], [(TMFn, hf)])
            TT("pool", RSF[:, sl], TMF[:, sl], MHALF[:, 0:4], ALU.pow, [(TMFn, hf), C], [(RSFn, hf)])
            for t in ts_:
                i = t % 2
                STT("dve", YO[:, i, :], X[:, t, :], RSF[:, t:t + 1], FGB, ALU.mult, ALU.mult, XK(t) + [(RSFn, hf), (FGn, 0)], [(YOn, i)])
                DMA("sp", yout[t * 128:(t + 1) * 128, :], YO[:, i, :], [(YOn, i)], [], "YO%d" % i, is_output=True)
                if pi == 0:
                    DMA("sp", X[:, t, :], D["xs"][t * 128:(t + 1) * 128, :], [], XK(t), "X%d" % t)

    for pi in range(2):
        if pi == 0:
            for t in range(8):
                DMA("sp", X[:, t, :], D["xp"][t * 128:(t + 1) * 128, :], [], XK(t), "X%d" % t)
        for l in range(2):
            stage("norm%d%d" % (pi, l))
            norm_phase(l, pi)
            stage("fourier%d%d" % (pi, l))
            fourier(pi, l)
            stage("pool%d%d" % (pi, l))
            poolmix(pi, l)
            stage("conv%d%d" % (pi, l))
            convmod(pi, l)
            stage("attn%d%d" % (pi, l))
            flush_mod()
            attention(pi, l)
        stage("final%d" % pi)
        final_norm(pi)


def build_nc():
    nc = bass.Bass("TRN2", target_bir_lowering=False)
    es = ExitStack()
    D = {}
    for k, shp in IN_SPECS.items():
        D[k] = nc.dram_tensor(k, list(shp), F32, kind="ExternalInput").ap()
    for k, (shp, dt) in CONST_SPECS.items():
        D[k] = nc.dram_tensor(k, list(shp), dt, kind="ExternalInput").ap()
    for k, shp in OUT_SPECS.items():
        D[k] = nc.dram_tensor(k, list(shp), F32, kind="ExternalOutput").ap()
    T = {"dram": D}

    def sb(name, shape, dt):
        T[name] = es.enter_context(nc.sbuf_tensor(name, list(shape), dt))

    sb("X", [128, 8, 2048], F32)
    sb("HT", [128, 16, 1024], BF16)
    sb("CAT", [128, 4, 1024], BF16)
    sb("RING", [128, 3, 4096], BF16)
    sb("GBC", [128, 2048], F32)
    sb("ARENA", [128, ARENA_BYTES // 2], BF16)
    sb("WTMP", [128, 2, 512], F32)
    sb("IDENT", [128, 128], BF16)
    sb("IDENTF", [128, 128], F32)
    sb("RPERM", [128, 128], F32)
    sb("ONESM", [128, 128], BF16)
    sb("CCSC", [128, 2, 128], BF16)
    sb("BAND", [128, 2560], BF16)
    sb("DFT256", [128, 2, 2, 256], BF16)
    sb("MHALF", [128, 8], F32)
    sb("SCT", [128, 16, 2], BF16)
    sb("CVT", [128, 2, 16], F32)
    sb("BMT", [128, 2, 48], F32)
    sb("GT_", [128, 2, 16], F32)
    sb("MODT", [128, 2, 48, 2], F32)
    sb("AT", [128, 2, 16, 2], F32)
    sb("TMPA", [128, 2, 16, 2], F32)
    for n in ("PST", "DWBT", "LNGT", "LNBT"):
        sb(n, [128, 2, 4], F32)
    sb("DWT", [128, 2, 4, 31], F32)
    sb("SGBC", [128, 2, 128], F32)
    sb("NLAM", [128, 2], F32)
    sb("EL", [128, 4], F32)
    sb("SL", [128, 4], F32)
    sb("NL0", [128, 2], F32)
    sb("SSQ", [128, 8], F32)
    sb("RSTD", [128, 8], F32)
    sb("TMP8", [128, 8], F32)
    T["PS"] = es.enter_context(nc.psum_tensor("PS", [128, 4096], F32))

    Pd = Prog(nc, es, dry=True)
    rd = Ring(Pd, T["RING"], plan=None)
    build_all(nc, es, T, Pd, rd)
    P = Prog(nc, es)
    ring = Ring(P, T["RING"], plan=rd.rec)
    build_all(nc, es, T, P, ring)
    assert ring.consumed == len(rd.rec)
    if STOP is not None:
        pass
    P.finalize()
    P.emit()
    return nc, es, P


_CACHE = {}


def kernel(x_prompt, x_sample, cache_k, cache_v, c, c_ctx, norm_g, w_mod, b_mod, w_in, w_fourier, w_pool, pool_scale,
           diff_lambda, subln_g, conv_dw, conv_dw_b, conv_ln_g, conv_ln_b, w_conv_pw, w_out, final_g):
    f = lambda a: np.ascontiguousarray(np.asarray(a, dtype=np.float32))
    if "nc" not in _CACHE:
        _CACHE["nc"] = build_nc()
        _CACHE["consts"] = make_consts()
    nc, es, P = _CACHE["nc"]
    consts = _CACHE["consts"]
    x_prompt, x_sample, cache_k, cache_v, c, c_ctx = map(f, (x_prompt, x_sample, cache_k, cache_v, c, c_ctx))
    shared = {
        "norm_g": f(norm_g), "w_mod": f(w_mod), "b_mod": f(b_mod), "w_in": f(w_in), "w_fourier": f(w_fourier),
        "w_pool": f(w_pool), "pool_scale": f(pool_scale), "diff_lambda": f(diff_lambda), "subln_g": f(subln_g),
        "conv_dw": f(conv_dw), "conv_dw_b": f(conv_dw_b), "conv_ln_g": f(conv_ln_g), "conv_ln_b": f(conv_ln_b),
        "w_conv_pw": f(w_conv_pw), "w_out": f(w_out), "final_g": f(final_g).reshape(1, 2048),
    }
    shared.update(consts)
    in_maps = []
    for i in range(8):
        m = dict(shared)
        m["xp"] = x_prompt[4 * i:4 * i + 4].reshape(1024, 2048)
        m["xs"] = x_sample[i]
        m["ck"] = cache_k[i]
        m["cv"] = cache_v[i]
        m["cvec"] = np.ascontiguousarray(np.stack([c_ctx, c[i]]))
        in_maps.append(m)
    res = run_bass_kernel_spmd(nc, in_maps, core_ids=list(range(8)))
    rs = res.results
    y_prompt = np.concatenate([r["yp"].reshape(4, 256, 2048) for r in rs], axis=0)
    y_sample = np.stack([r["ys"] for r in rs], axis=0)
    nk = np.concatenate([r["nk"] for r in rs], axis=0)
    nv = np.concatenate([r["nv"] for r in rs], axis=0)
    return (y_prompt.astype(np.float32), y_sample.astype(np.float32), nk.astype(np.float32), nv.astype(np.float32))
```

```python
import math
from contextlib import ExitStack

import ml_dtypes
import numpy as np

import concourse.bass as bass
import concourse.mybir as mybir
from concourse.bass_utils import run_bass_kernel_spmd

F32 = mybir.dt.float32
BF16 = mybir.dt.bfloat16
AF = mybir.ActivationFunctionType
ALU = mybir.AluOpType
AX = mybir.AxisListType
NPBF = ml_dtypes.bfloat16

ENGS = ("pe", "act", "dve", "pool", "sp")
EPS = 1e-6
POOL_WINDOWS = (2, 4, 8, 16)
LAM_INIT = [0.8 - 0.6 * math.exp(-0.3 * l) for l in range(2)]
KB = 1024


class Op:
    __slots__ = ("eng", "fn", "reads", "writes", "dma", "waits", "signal", "eidx", "tick", "semval",
                 "prev_same_sem", "seq")

    def __init__(self, eng, fn, reads, writes, dma):
        self.eng = eng
        self.fn = fn
        self.reads = tuple(reads)
        self.writes = tuple(writes)
        self.dma = dma
        self.waits = []
        self.signal = False
        self.eidx = 0
        self.tick = 0
        self.semval = 0
        self.prev_same_sem = None


class Prog:
    def __init__(self, nc, es, dry=False):
        self.nc = nc
        self.es = es
        self.dry = dry
        self.ops = []
        self.bufacc = {}
        self.alias = {}
        self.live = []
        self.out_sems = set()
        self.last_dma_on_sem = {}

    def op(self, eng, fn, reads=(), writes=(), dma=None, is_output=False):
        if self.dry:
            return None
        o = Op(eng, fn, reads, writes, dma)
        o.seq = len(self.ops)
        self.ops.append(o)
        for k in o.reads + o.writes:
            d = self.bufacc.setdefault(k[0], {})
            if dma:
                d.setdefault("dma", []).append(o)
            else:
                d[eng] = o
        if dma:
            if not dma.startswith("G:"):
                o.prev_same_sem = self.last_dma_on_sem.get(dma)
                self.last_dma_on_sem[dma] = o
            if is_output:
                self.out_sems.add(dma)
        return o

    def arena_alloc(self, name, off, size):
        if self.dry:
            return
        end = off + size
        deps = []
        keep = []
        for (n, a, b) in self.live:
            if a < end and off < b:
                acc = self.bufacc.get(n, {})
                for e, o in acc.items():
                    if e == "dma":
                        deps.extend(o)
                    else:
                        deps.append(o)
                deps.extend(self.alias.get(n, ()))
            else:
                keep.append((n, a, b))
        keep.append((name, off, end))
        self.live = keep
        assert name not in self.bufacc, name
        best = {}
        dmas = []
        for o in deps:
            if o.dma:
                dmas.append(o)
            elif o.eng not in best or best[o.eng].seq < o.seq:
                best[o.eng] = o
        self.alias[name] = list(best.values()) + dmas

    def finalize(self):
        eidx = {e: 0 for e in ENGS}
        semcnt = {}
        for o in self.ops:
            eidx[o.eng] += 1
            o.eidx = eidx[o.eng]
            if o.dma:
                semcnt[o.dma] = semcnt.get(o.dma, 0) + 16
                o.semval = semcnt[o.dma]
        self.semtotal = semcnt
        last_w = {}
        readers = {}
        known = {e: {} for e in ENGS}
        for o in self.ops:
            deps = []
            for k in o.reads:
                w = last_w.get(k)
                if w is not None:
                    deps.append(w)
            for k in o.writes:
                w = last_w.get(k)
                if w is not None:
                    deps.append(w)
                r = readers.get(k)
                if r:
                    for e, v in r.items():
                        if e == "dma":
                            deps.extend(v)
                        else:
                            deps.append(v)
            names = set(k[0] for k in o.reads + o.writes)
            for n in names:
                al = self.alias.get(n)
                if al:
                    deps.extend(al)
            if o.prev_same_sem is not None:
                deps.append(o.prev_same_sem)
            kn = known[o.eng]
            engw = {}
            dmaw = {}
            for d in deps:
                if d is o:
                    continue
                if d.dma:
                    if o.dma == d.dma and d.dma.startswith("G:"):
                        continue
                    val = semcnt[d.dma] if d.dma.startswith("G:") else d.semval
                    key = "D:" + d.dma
                    if kn.get(key, 0) >= val:
                        continue
                    kn[key] = val
                    dmaw[d.dma] = max(dmaw.get(d.dma, 0), val)
                else:
                    if d.eng == "pe" and o.eng == "pe":
                        continue
                    key = "E:" + d.eng
                    if kn.get(key, 0) >= d.eidx:
                        continue
                    kn[key] = d.eidx
                    if d.eng not in engw or engw[d.eng].eidx < d.eidx:
                        engw[d.eng] = d
            for d in engw.values():
                d.signal = True
            o.waits = [("eng", d) for d in engw.values()] + [("dma", s, v) for s, v in dmaw.items()]
            for k in o.reads:
                r = readers.setdefault(k, {})
                if o.dma:
                    r.setdefault("dma", []).append(o)
                else:
                    r[o.eng] = o
            for k in o.writes:
                last_w[k] = o
                readers[k] = {}
        tick = {e: 0 for e in ENGS}
        for o in self.ops:
            if o.dma is None and o.signal:
                tick[o.eng] += 1
                o.tick = tick[o.eng]
        self.nticks = tick

    def emit(self):
        nc = self.nc
        handles = {"pe": nc.tensor, "act": nc.scalar, "dve": nc.vector, "pool": nc.gpsimd, "sp": nc.sync}
        esem = {e: self.es.enter_context(nc.semaphore("sem_" + e)) for e in ENGS}
        dsem = {}

        def getd(name):
            if name not in dsem:
                dsem[name] = self.es.enter_context(nc.semaphore("d_" + name.replace(":", "_")))
            return dsem[name]

        nwait = 0
        for o in self.ops:
            E = handles[o.eng]
            for w in o.waits:
                nwait += 1
                if w[0] == "eng":
                    d = w[1]
                    E.wait_ge(esem[d.eng], d.tick)
                else:
                    E.wait_ge(getd(w[1]), w[2])
            ins = o.fn(E)
            if o.dma:
                ins.then_inc(getd(o.dma), 16)
            elif o.signal:
                ins.then_inc(esem[o.eng], 1)
        for s in sorted(self.semtotal):
            nc.sync.wait_ge(getd(s), self.semtotal[s])
        for e in ENGS:
            if e != "sp" and self.nticks[e] > 0:
                nc.sync.wait_ge(esem[e], self.nticks[e])
        self.stats = dict(nops=len(self.ops), nwait=nwait, ticks=self.nticks, nsem=len(dsem) + len(esem))


def make_consts():
    c = {}

    def dft(S):
        s = np.arange(S, dtype=np.int64)
        ang = 2.0 * np.pi * ((np.outer(s, s) % S).astype(np.float64) / S)
        nrm = 1.0 / math.sqrt(S * 128.0)
        return np.stack([np.cos(ang) * nrm, np.sin(ang) * nrm]).astype(NPBF)

    c["c_dft256"] = dft(256)
    c["c_dft1024"] = dft(1024)
    s = np.arange(128, dtype=np.int64)
    ang = 2.0 * np.pi * ((np.outer(s, s) % 128).astype(np.float64) / 128)
    c["c_ccsc"] = np.stack([np.cos(ang), -np.sin(ang)]).astype(NPBF)
    S = 1024
    band = np.zeros((4, 5, 128, 128), np.float64)
    for g, w in enumerate(POOL_WINDOWS):
        B = np.zeros((S, S), np.float64)
        for t in range(S):
            lo = min(max(t - w // 2, 0), S)
            hi = min(max(t - w // 2 + w, 0), S)
            B[lo:hi, t] = 1.0 / (hi - lo)
            B[t, t] -= 1.0
        band[g, 0] = B[0:128, 0:128]
        band[g, 1] = B[128:256, 128:256]
        band[g, 2] = B[896:1024, 896:1024]
        band[g, 3] = B[0:128, 128:256]
        band[g, 4] = B[128:256, 0:128]
    c["c_band"] = np.ascontiguousarray(band.transpose(2, 0, 1, 3).reshape(128, 4 * 5 * 128)).astype(NPBF)
    c["c_ident"] = np.eye(128).astype(NPBF)
    c["c_identf"] = np.eye(128).astype(np.float32)
    c["c_onesm"] = np.full((128, 128), 1.0 / 512.0).astype(NPBF)
    c["c_mhalf"] = np.full((128, 8), -0.5, np.float32)
    inv = (np.float32(10000.0) ** (-np.arange(16, dtype=np.float32) / np.float32(16))).astype(np.float32)
    tok = np.arange(1024)
    pos = [np.floor_divide(tok, 64).astype(np.float32), np.mod(tok, 64).astype(np.float32)]
    rope = np.zeros((128, 2, 1024), np.float32)
    rperm = np.zeros((128, 128), np.float32)
    for p in range(128):
        a = (p % 64) // 32
        hh = (p % 32) // 16
        f = p % 16
        angp = (pos[a] * inv[f]).astype(np.float32)
        rope[p, 0] = np.cos(angp.astype(np.float64))
        rope[p, 1] = np.sin(angp.astype(np.float64)) * (-1.0 if hh == 0 else 1.0)
        partner = p + 16 if hh == 0 else p - 16
        rperm[partner, p] = 1.0
    c["c_rope"] = rope
    c["c_rperm"] = rperm
    return c


CONST_SPECS = {
    "c_dft256": ([2, 256, 256], BF16), "c_dft1024": ([2, 1024, 1024], BF16), "c_ccsc": ([2, 128, 128], BF16),
    "c_band": ([128, 2560], BF16), "c_ident": ([128, 128], BF16), "c_identf": ([128, 128], F32),
    "c_onesm": ([128, 128], BF16), "c_mhalf": ([128, 8], F32), "c_rope": ([128, 2, 1024], F32),
    "c_rperm": ([128, 128], F32),
}
IN_SPECS = {
    "xp": [1024, 2048], "xs": [1024, 2048], "ck": [2, 4, 256, 128], "cv": [2, 4, 256, 128], "cvec": [2, 2048],
    "norm_g": [2, 2048], "w_mod": [2, 2048, 6144], "b_mod": [2, 6144], "w_in": [2, 2048, 5632],
    "w_fourier": [2, 512, 512], "w_pool": [2, 4, 128, 128], "pool_scale": [2, 512], "diff_lambda": [2, 4, 64],
    "subln_g": [2, 128], "conv_dw": [2, 31, 512], "conv_dw_b": [2, 512], "conv_ln_g": [2, 512],
    "conv_ln_b": [2, 512], "w_conv_pw": [2, 512, 512], "w_out": [2, 2048, 2048], "final_g": [1, 2048],
}
OUT_SPECS = {"yp": [1024, 2048], "ys": [1024, 2048], "nk": [4, 2, 4, 256, 128], "nv": [4, 2, 4, 256, 128]}
ARENA_BYTES = 53 * KB


class Ring:
    def __init__(self, P, RING, plan=None):
        self.P = P
        self.RING = RING
        self.plan = plan
        self.rec = []
        self.issued = 0
        self.consumed = 0
        self.posted = 0
        self.handlers = {}

    def view(self, i, n):
        return self.RING[:, i % 3, :].rearrange("p (k n) -> p k n", n=n), ("ring", i % 3)

    def _issue(self, i):
        slot = i % 3
        src, k, n, post = self.plan[i]
        dst = self.RING[:, slot, :].rearrange("p (k n) -> p k n", n=n)
        self.P.op("pool", lambda e: e.dma_start(out=dst, in_=src), writes=[("ring", slot)], dma="ring%d" % slot)

    def _post(self, upto):
        while self.posted <= min(upto, len(self.plan) - 1):
            i = self.posted
            src, k, n, post = self.plan[i]
            if post is not None:
                v, rk = self.view(i, n)
                self.handlers[post[0]](v, rk, *post[1:])
            self.posted += 1

    def next(self, src, k, n, post=None):
        assert k * n == 4096
        i = self.consumed
        self.consumed += 1
        if self.plan is None:
            self.rec.append((src, k, n, post))
            return self.RING[:, 0, :].rearrange("p (k n) -> p k n", n=n), ("ring", 0)
        while self.issued <= min(i + 2, len(self.plan) - 1):
            self._issue(self.issued)
            self.issued += 1
        self._post(min(i + 1, self.issued - 1))
        return self.view(i, n)


class _Stop(Exception):
    pass


STOP = None


def stage(name):
    if STOP is not None and name == STOP:
        raise _Stop()


def build_all(nc, es, T, P, ring):
    try:
        _build_all(nc, es, T, P, ring)
    except _Stop:
        pass


def _build_all(nc, es, T, P, ring):
    X, HT, CAT, GBC, ARENA, PS, WTMP = T["X"], T["HT"], T["CAT"], T["GBC"], T["ARENA"], T["PS"], T["WTMP"]
    IDENT, IDENTF, RPERM, ONESM, CCSC, BAND, DFT256, MHALF = (T[k] for k in
                                                             ("IDENT", "IDENTF", "RPERM", "ONESM", "CCSC", "BAND", "DFT256", "MHALF"))
    SCT, CVT, BMT, GT_, MODT, AT, TMPA = (T[k] for k in ("SCT", "CVT", "BMT", "GT_", "MODT", "AT", "TMPA"))
    PST, DWBT, LNGT, LNBT, DWT, SGBC, NLAM = (T[k] for k in ("PST", "DWBT", "LNGT", "LNBT", "DWT", "SGBC", "NLAM"))
    SSQ, RSTD, TMP8 = T["SSQ"], T["RSTD"], T["TMP8"]
    D = T["dram"]
    C = ("c", 0)
    uid = [0]

    class PsA:
        def __init__(self):
            self.pool = list(range(8))
            self.n = 0

        def set_pool(self, banks):
            self.pool = list(banks)
            self.n = 0

        def get(self):
            b = self.pool[self.n % len(self.pool)]
            self.n += 1
            return b

    psa = PsA()

    def PSb(b):
        return PS[:, b * 512:(b + 1) * 512]

    def arena(base, off, nbytes, dt, shape=None):
        assert off % 4 == 0 and off + nbytes <= ARENA_BYTES, (base, off, nbytes)
        uid[0] += 1
        name = "%s#%d" % (base, uid[0])
        P.arena_alloc(name, off, nbytes)
        ap = ARENA[:, off // 2:(off + nbytes) // 2]
        if dt == F32:
            ap = ap.bitcast(F32)
        if shape is not None and len(shape) == 2:
            ap = ap.rearrange("p (a b) -> p a b", b=shape[1])
        elif shape is not None and len(shape) == 3:
            ap = ap.rearrange("p (a b c) -> p a b c", b=shape[1], c=shape[2])
        return ap, name

    def MM(out, lhsT, rhs, st, sp, r, w):
        P.op("pe", lambda e: e.matmul(out=out, lhsT=lhsT, rhs=rhs, start=st, stop=sp), r, w)

    def TRN(out, in_, r, w):
        P.op("pe", lambda e: e.transpose(out=out, in_=in_, identity=IDENT[:]), list(r) + [C], w)

    def ACTV(out, in_, func, r, w, **kw):
        P.op("act", lambda e: e.activation(out=out, in_=in_, func=func, **kw), r, w)

    def TT(eng, out, in0, in1, op, r, w):
        P.op(eng, lambda e: e.tensor_tensor(out=out, in0=in0, in1=in1, op=op), r, w)

    def TS(eng, out, in0, s1, s2, op0, op1, r, w):
        if s2 is None:
            P.op(eng, lambda e: e.tensor_scalar(out=out, in0=in0, scalar1=s1, scalar2=None, op0=op0), r, w)
        else:
            P.op(eng, lambda e: e.tensor_scalar(out=out, in0=in0, scalar1=s1, scalar2=s2, op0=op0, op1=op1), r, w)

    def STT(eng, out, in0, scalar, in1, op0, op1, r, w):
        P.op(eng, lambda e: e.scalar_tensor_tensor(out=out, in0=in0, scalar=scalar, in1=in1, op0=op0, op1=op1), r, w)

    def CP(eng, out, in_, r, w):
        if eng == "act":
            P.op(eng, lambda e: e.copy(out=out, in_=in_), r, w)
        else:
            P.op(eng, lambda e: e.tensor_copy(out=out, in_=in_), r, w)

    def DMA(eng, out, in_, r, w, sem, is_output=False, slow=False):
        if slow:
            P.op(eng, lambda e: e.dma_start(out=out, in_=in_, allow_slow_non_contiguous=True), r, w, dma=sem, is_output=is_output)
        else:
            P.op(eng, lambda e: e.dma_start(out=out, in_=in_), r, w, dma=sem, is_output=is_output)

    def MEMSET(eng, ap, val, w):
        P.op(eng, lambda e: e.memset(ap, val), [], w)

    def bc_last(ap, shape):
        return ap.broadcast_to(list(shape))

    G = "G:const"
    import os
    SKIP = os.environ.get("KSKIP", "").split(",")
    def CD(tag, out, in_, slow=False):
        if tag in SKIP:
            return
        DMA("sp", out, in_, [], [C], G, slow=slow)

    CD("id0", IDENT[:], D["c_ident"])
    CD("id1", IDENTF[:], D["c_identf"])
    CD("id2", RPERM[:], D["c_rperm"])
    CD("id3", ONESM[:], D["c_onesm"])
    CD("id4", MHALF[:], D["c_mhalf"])
    CD("ccsc", CCSC[:], D["c_ccsc"].rearrange("t p n -> p t n"))
    CD("band", BAND[:], D["c_band"])
    for t_ in range(2):
        CD("dft", DFT256[:, t_, :, :], D["c_dft256"][t_].rearrange("(k p) n -> p k n", p=128))
    CD("cvt", CVT[:], D["cvec"].rearrange("v (k p) -> p v k", p=128), slow=True)
    CD("bmt", BMT[:], D["b_mod"].rearrange("l (j p) -> p l j", p=128), slow=True)
    CD("gt", GT_[:], D["norm_g"].rearrange("l (k p) -> p l k", p=128), slow=True)
    for sbt, nm in ((PST, "pool_scale"), (DWBT, "conv_dw_b"), (LNGT, "conv_ln_g"), (LNBT, "conv_ln_b")):
        CD("vec4", sbt[:], D[nm].rearrange("l (j p) -> p l j", p=128), slow=True)
    for l in range(2):
        for j in range(4):
            CD("dwt", DWT[:, l, j, :], D["conv_dw"][l, :, j * 128:(j + 1) * 128].rearrange("t p -> p t"), slow=True)
        CD("sgbc", SGBC[:, l, :], D["subln_g"][l:l + 1, :].partition_broadcast(128))

    stage("p1")
    ACTV(SCT[:].rearrange("p k v -> p v k"), CVT[:], AF.Silu, [C], [("SCT", 0)])

    stage("p2")
    pending_mod = []

    def mod_units(l, bank):
        MODP = PSb(bank)[:, 0:96]
        fs = []

        def one(u):
            unit, rk = ring.next(D["w_mod"][l][:, u * 256:(u + 1) * 256].rearrange("(k p) n -> p k n", p=128), 16, 256)
            for cg in range(2):
                jc = u * 2 + cg
                for kc in range(16):
                    MM(MODP[:, jc * 2:jc * 2 + 2], unit[:, kc, cg * 128:(cg + 1) * 128], SCT[:, kc, :], kc == 0, kc == 15,
                       [rk, ("SCT", 0)], [("ps", bank)])
            if u == 23:
                TT("dve", MODT[:, l, :, :], MODP.rearrange("p (j v) -> p j v", v=2),
                   BMT[:, l, :].unsqueeze(2).broadcast_to([128, 48, 2]), ALU.add, [("ps", bank), C], [("MODT", l)])
                TS("dve", TMPA[:, l, :, :], MODT[:, l, 16:32, :], 1.0, None, ALU.add, None, [("MODT", l)], [("TMPA", l)])
                TT("dve", AT[:, l, :, :], TMPA[:, l, :, :], GT_[:, l, :].unsqueeze(2).broadcast_to([128, 16, 2]), ALU.mult,
                   [("TMPA", l), C], [("AT", l)])

        for u in range(24):
            fs.append(lambda u=u: one(u))
        return fs

    def interleave(k=1):
        for _ in range(k):
            if pending_mod:
                pending_mod.pop(0)()

    def flush_mod():
        while pending_mod:
            pending_mod.pop(0)()
        psa.set_pool(range(8))

    for f_ in mod_units(0, 0):
        f_()
    pending_mod.extend(mod_units(1, 7))
    psa.set_pool(range(1, 7))
    stage("p4")
    EL, SL, NL0 = T["EL"], T["SL"], T["NL0"]
    DL, DLn = arena("DL", 0, 2 * KB, F32)
    PR, PRn = arena("PR", 2 * KB, 1 * KB, F32, (4, 64))
    for l in range(2):
        DMA("sp", DL[:, l * 256:(l + 1) * 256],
            D["diff_lambda"].rearrange("l a d -> l (a d)")[l:l + 1, :].partition_broadcast(128), [], [(DLn, l)], "DL")
    DLv = DL.rearrange("p (a w d) -> p a w d", w=2, d=64)
    TT("dve", PR, DLv[:, :, 0, :], DLv[:, :, 1, :], ALU.mult, [(DLn, 0), (DLn, 1)], [(PRn, 0)])
    P.op("dve", lambda e: e.tensor_reduce(out=SL[:], in_=PR, axis=AX.X, op=ALU.add), [(PRn, 0)], [("SL", 0)])
    ACTV(EL[:], SL[:], AF.Exp, [("SL", 0)], [("EL", 0)])
    ELv = EL[:].rearrange("p (l a) -> p l a", a=2)
    TT("dve", NL0[:], ELv[:, :, 1], ELv[:, :, 0], ALU.subtract, [("EL", 0)], [("NL0", 0)])
    for l in range(2):
        TS("dve", NLAM[:, l:l + 1], NL0[:, l:l + 1], -LAM_INIT[l], None, ALU.add, None, [("NL0", 0)], [("NLAM", 0)])
        TS("dve", SGBC[:, l, :], SGBC[:, l, :], 1.0 - LAM_INIT[l], None, ALU.mult, None, [C], [("SGBC", 0)])

    stage("p5")
    XK = lambda t: [("X", t * 4 + i) for i in range(4)]

    def proj_tm(l, chunk, evac):
        for hf in range(2):
            c0 = chunk * 512 + hf * 256
            unit, rk = ring.next(D["w_in"][l][:, c0:c0 + 256].rearrange("(k p) n -> p k n", p=128), 16, 256)
            for tp in range(4):
                b = psa.get()
                for tt in range(2):
                    t = 2 * tp + tt
                    for kc in range(16):
                        MM(PSb(b)[:, tt * 256:(tt + 1) * 256], HT[:, kc, t * 128:(t + 1) * 128], unit[:, kc, :],
                           kc == 0, kc == 15, [rk, ("HT", kc)], [("ps", b)])
                evac(hf, tp, b)
            interleave(1)

    def proj_fm(l, chunk, evac, mid=None):
        for hf in range(2):
            if hf == 1 and mid is not None:
                mid()
            c0 = chunk * 512 + hf * 256
            unit, rk = ring.next(D["w_in"][l][:, c0:c0 + 256].rearrange("(k p) n -> p k n", p=128), 16, 256)
            for cg in range(2):
                j = hf * 2 + cg
                for tb in range(2):
                    b = psa.get()
                    for kc in range(16):
                        MM(PSb(b), unit[:, kc, cg * 128:(cg + 1) * 128], HT[:, kc, tb * 512:(tb + 1) * 512],
                           kc == 0, kc == 15, [rk, ("HT", kc)], [("ps", b)])
                    evac(j, tb, b)
            interleave(1)

    def fold(v, rk_, nh):
        TT("pool", v, v, GBC[:, nh * 1024:(nh + 1) * 1024].unsqueeze(1).broadcast_to([128, 4, 1024]), ALU.mult,
           [rk_, ("GBC", 0)], [rk_])

    ring.handlers["fold"] = fold

    def wout(l, bi):
        for nh in range(2):
            src = D["w_out"][l][bi * 512:(bi + 1) * 512, nh * 1024:(nh + 1) * 1024].rearrange("(k p) n -> p k n", p=128)
            unit, rk = ring.next(src, 4, 1024, post=("fold", nh))
            for t in range(8):
                for nb in range(2):
                    b = psa.get()
                    blk = nh * 2 + nb
                    c0 = blk * 512
                    for kc in range(4):
                        MM(PSb(b), CAT[:, kc, t * 128:(t + 1) * 128], unit[:, kc, nb * 512:(nb + 1) * 512],
                           kc == 0, kc == 3, [rk, ("CAT", kc)], [("ps", b)])
                    TT("dve", X[:, t, c0:c0 + 512], PSb(b), X[:, t, c0:c0 + 512], ALU.add,
                       [("ps", b), ("X", t * 4 + blk)], [("X", t * 4 + blk)])
            interleave(1)

    RK = [("RSTD", p) for p in range(4)]
    TK = [("TMP8", p) for p in range(4)]

    def rstd_all(nelem):
        TS("dve", TMP8[:], SSQ[:], 1.0 / nelem, EPS, ALU.mult, ALU.add, [("SSQ", t) for t in range(8)], TK)
        TT("pool", RSTD[:], TMP8[:], MHALF[:], ALU.pow, TK + [C], RK)

    def norm_phase(l, v):
        H, Hn = arena("H", 0, 32 * KB, BF16, (8, 2048))
        JUNK, Jn = arena("JUNK", 32 * KB, 4 * KB, BF16)
        def sq(t):
            ACTV(JUNK, X[:, t, :], AF.Square, XK(t), [(Jn, 0), ("SSQ", t)], accum_out=SSQ[:, t:t + 1])

        def rs(p):
            sl = slice(2 * p, 2 * p + 2)
            TS("pool", TMP8[:, sl], SSQ[:, sl], 1.0 / 2048.0, EPS, ALU.mult, ALU.add, [("SSQ", 2 * p), ("SSQ", 2 * p + 1)], [TK[p]])
            TT("pool", RSTD[:, sl], TMP8[:, sl], MHALF[:, 0:2], ALU.pow, [TK[p], C], [RK[p]])

        def idt(t):
            ACTV(H[:, t, :], X[:, t, :], AF.Identity, XK(t) + [RK[t // 2]], [(Hn, t)], scale=RSTD[:, t:t + 1])

        sq(0); sq(1); rs(0)
        sq(2); sq(3); rs(1)
        idt(0); idt(1)
        sq(4); sq(5); rs(2)
        idt(2); idt(3)
        sq(6); sq(7); rs(3)
        idt(4); idt(5); idt(6); idt(7)
        GTB, Gn = arena("GTB", 36 * KB, 8 * KB, F32, (16, 128))
        for kc in range(16):
            b = psa.get()
            pb = PSb(b).bitcast(BF16)
            for t in range(8):
                TRN(pb[:, t * 128:(t + 1) * 128], H[:, t, kc * 128:(kc + 1) * 128], [(Hn, t)], [("ps", b)])
            if kc % 2 == 0:
                TS("dve", HT[:, kc, :], pb, AT[:, l, kc, v:v + 1], MODT[:, l, kc, v:v + 1], ALU.mult, ALU.add,
                   [("ps", b), ("AT", l), ("MODT", l)], [("HT", kc)])
            else:
                ACTV(HT[:, kc, :], pb, AF.Identity, [("ps", b), ("AT", l), ("MODT", l)], [("HT", kc)],
                     scale=AT[:, l, kc, v:v + 1], bias=MODT[:, l, kc, v:v + 1])
        CP("dve", GTB, MODT[:, l, 32:48, v:v + 1].broadcast_to([128, 16, 128]), [("MODT", l)], [(Gn, 0)])
        for q4 in range(4):
            b = psa.get()
            for i in range(4):
                kc = q4 * 4 + i
                MM(PSb(b)[:, i * 128:(i + 1) * 128], GTB[:, kc, :], IDENTF[:], True, True, [(Gn, 0), C], [("ps", b)])
            CP("act", GBC[:, q4 * 512:(q4 + 1) * 512], PSb(b), [("ps", b)], [("GBC", 0)])

    def fourier(pi, l):
        NS, S = (4, 256) if pi == 0 else (1, 1024)
        FG, FGn = arena("FG", 0, 8 * KB, BF16, (4, 1024))
        FX, FXn = arena("FX", 8 * KB, 8 * KB, BF16, (8, 512))
        PT, PTn = arena("PT", 16 * KB, 16 * KB, BF16, (2, 4, 1024))
        WF, WFn = arena("WF", 32 * KB, 4 * KB, BF16, (4, 512))
        WCS, WCn = arena("WCS", 36 * KB, 8 * KB, BF16, (2, 4, 512))
        DMA("pool", WF, D["w_fourier"][l].rearrange("(g c) d -> c g d", c=128), [], [(WFn, 0)], "WF")

        def ev_fg(j, tb, b):
            ACTV(FG[:, j, tb * 512:(tb + 1) * 512], PSb(b), AF.Silu, [("ps", b)], [(FGn, j * 2 + tb)])

        proj_fm(l, 1, ev_fg)

        def ev_fx(hf, tp, b):
            CP("act", FX[:, 2 * tp:2 * tp + 2, hf * 256:(hf + 1) * 256], PSb(b).rearrange("p (t n) -> p t n", n=256),
               [("ps", b)], [(FXn, 2 * tp), (FXn, 2 * tp + 1)])

        proj_tm(l, 0, ev_fx)
        for cs in range(2):
            for g in range(4):
                b = psa.get()
                MM(PSb(b), CCSC[:, cs, :], WF[:, g, :], True, True, [C, (WFn, 0)], [("ps", b)])
                CP("dve", WCS[:, cs, g, :], PSb(b), [("ps", b)], [(WCn, cs * 4 + g)])
        if pi == 0:
            for n in range(4):
                for cs in range(2):
                    for gp in range(2):
                        b = psa.get()
                        for gg in range(2):
                            g = gp * 2 + gg
                            for kt in range(2):
                                MM(PSb(b)[:, gg * 256:(gg + 1) * 256], FX[:, n * 2 + kt, g * 128:(g + 1) * 128],
                                   DFT256[:, cs, kt, :], kt == 0, kt == 1, [(FXn, n * 2 + kt), C], [("ps", b)])
                        CP("act", PT[:, cs, gp * 2:gp * 2 + 2, n * 256:(n + 1) * 256],
                           PSb(b).rearrange("p (g n) -> p g n", n=256), [("ps", b)], [(PTn, cs * 2 + n // 2)])
        else:
            for cs in range(2):
                for half in range(2):
                    src = D["c_dft1024"][cs][:, half * 512:(half + 1) * 512].rearrange("(k p) n -> p k n", p=128)
                    unit, rk = ring.next(src, 8, 512)
                    for g in range(4):
                        b = psa.get()
                        for kt in range(8):
                            MM(PSb(b), FX[:, kt, g * 128:(g + 1) * 128], unit[:, kt, :], kt == 0, kt == 7,
                               [(FXn, kt), rk], [("ps", b)])
                        CP("act", PT[:, cs, g, half * 512:(half + 1) * 512], PSb(b), [("ps", b)], [(PTn, cs * 2 + half)])
        for j in range(4):
            for tb in range(2):
                b = psa.get()
                i = 0
                for cs in range(2):
                    for g in range(4):
                        MM(PSb(b), WCS[:, cs, g, j * 128:(j + 1) * 128], PT[:, cs, g, tb * 512:(tb + 1) * 512],
                           i == 0, i == 7, [(WCn, cs * 4 + g), (PTn, cs * 2 + tb)], [("ps", b)])
                        i += 1
                TT("dve", CAT[:, j, tb * 512:(tb + 1) * 512], PSb(b), FG[:, j, tb * 512:(tb + 1) * 512], ALU.mult,
                   [("ps", b), (FGn, j * 2 + tb)], [("CAT", j)])
        wout(l, 0)

    def poolmix(pi, l):
        NS, S = (4, 256) if pi == 0 else (1, 1024)
        NB = S // 128
        PG, PGn = arena("PG", 0, 8 * KB, BF16, (4, 1024))
        PX, PXn = arena("PX", 8 * KB, 8 * KB, BF16, (8, 512))
        PL, PLn = arena("PL", 16 * KB, 8 * KB, BF16, (4, 1024))
        WP, WPn = arena("WP", 24 * KB, 1 * KB, BF16, (4, 128))
        BANDv = BAND[:].rearrange("p (g t s) -> p g t s", t=5, s=128)
        DMA("pool", WP, D["w_pool"][l].rearrange("g c d -> c g d"), [], [(WPn, 0)], "WP")

        def ev_pg(j, tb, b):
            ACTV(PG[:, j, tb * 512:(tb + 1) * 512], PSb(b), AF.Silu, [("ps", b)], [(PGn, j * 2 + tb)])

        proj_fm(l, 3, ev_pg)

        def ev_px(hf, tp, b):
            CP("act", PX[:, 2 * tp:2 * tp + 2, hf * 256:(hf + 1) * 256], PSb(b).rearrange("p (t n) -> p t n", n=256),
               [("ps", b)], [(PXn, 2 * tp), (PXn, 2 * tp + 1)])

        proj_tm(l, 2, ev_px)
        for n in range(NS):
            for g in range(4):
                for grp in range((NB + 3) // 4):
                    b = psa.get()
                    nblk = min(4, NB - grp * 4)
                    for jl in range(nblk):
                        jt = grp * 4 + jl
                        srcs = []
                        if jt > 0:
                            srcs.append((jt - 1, 3))
                        srcs.append((jt, 0 if jt == 0 else (2 if jt == NB - 1 else 1)))
                        if jt < NB - 1:
                            srcs.append((jt + 1, 4))
                        for i, (kt, ty) in enumerate(srcs):
                            tile = n * NB + kt
                            MM(PSb(b)[:, jl * 128:(jl + 1) * 128], PX[:, tile, g * 128:(g + 1) * 128], BANDv[:, g, ty, :],
                               i == 0, i == len(srcs) - 1, [(PXn, tile), C], [("ps", b)])
                    t0 = n * S + grp * 512
                    CP("act", PL[:, g, t0:t0 + nblk * 128], PSb(b)[:, 0:nblk * 128], [("ps", b)], [(PLn, g * 2 + t0 // 512)])
        for g in range(4):
            for tb in range(2):
                b = psa.get()
                MM(PSb(b), WP[:, g, :], PL[:, g, tb * 512:(tb + 1) * 512], True, True, [(WPn, 0), (PLn, g * 2 + tb)], [("ps", b)])
                STT("dve", CAT[:, g, tb * 512:(tb + 1) * 512], PSb(b), PST[:, l, g:g + 1], PG[:, g, tb * 512:(tb + 1) * 512],
                    ALU.mult, ALU.mult, [("ps", b), C, (PGn, g * 2 + tb)], [("CAT", g)])
        wout(l, 1)

    def convmod(pi, l):
        NS, S = (4, 256) if pi == 0 else (1, 1024)
        SP = S + 30
        UPb, UPn = arena("UP", 0, 9 * KB, BF16)
        DGs = [arena("DG0", 9 * KB, 8 * KB, BF16), arena("DG1", 17 * KB, 8 * KB, BF16)]
        Y, Yn = arena("Y", 25 * KB, 16 * KB, F32, (4, 1024))
        SG, SGn = arena("SG", 41 * KB, 8 * KB, BF16, (4, 1024))
        WPW, WWn = arena("WPW", 49 * KB, 4 * KB, BF16, (4, 512))
        DMA("pool", WPW, D["w_conv_pw"][l].rearrange("(j c) d -> c j d", c=128), [], [(WWn, 0)], "WPW")
        UP = UPb[:, 0:4 * NS * SP].rearrange("p (j n s) -> p j n s", n=NS, s=SP)
        MEMSET("pool", UPb, 0.0, [(UPn, 0)])

        def ev_sg(j, tb, b):
            ACTV(SG[:, j, tb * 512:(tb + 1) * 512], PSb(b), AF.Sigmoid, [("ps", b)], [(SGn, j * 2 + tb)])

        proj_fm(l, 9, ev_sg)

        def ev_u(j, tb, b):
            if pi == 0:
                TT("dve", UP[:, j, 2 * tb:2 * tb + 2, 15:15 + 256], PSb(b).rearrange("p (n s) -> p n s", s=256),
                   SG[:, j, tb * 512:(tb + 1) * 512].rearrange("p (n s) -> p n s", s=256), ALU.mult,
                   [("ps", b), (SGn, j * 2 + tb), (UPn, 0)], [(UPn, 0)])
            else:
                TT("dve", UP[:, j, 0, 15 + tb * 512:15 + (tb + 1) * 512], PSb(b), SG[:, j, tb * 512:(tb + 1) * 512], ALU.mult,
                   [("ps", b), (SGn, j * 2 + tb), (UPn, 0)], [(UPn, 0)])

        proj_fm(l, 8, ev_u)
        NPE = 26
        for j in range(4):
            DG, DGn = DGs[j % 2]
            DGv = DG[:, 0:NPE * 128].rearrange("p (t c) -> p t c", c=128)
            TT("pool", DGv, IDENT[:].unsqueeze(1).broadcast_to([128, NPE, 128]),
               DWT[:, l, j, 0:NPE].unsqueeze(2).broadcast_to([128, NPE, 128]), ALU.mult, [C], [(DGn, 0)])

            def yblk(tb):
                if pi == 0:
                    return Y[:, j, tb * 512:(tb + 1) * 512].rearrange("p (n s) -> p n s", s=256)
                return Y[:, j, tb * 512:(tb + 1) * 512]

            def ushift(tb, tap):
                if pi == 0:
                    return UP[:, j, 2 * tb:2 * tb + 2, tap:tap + 256]
                return UP[:, j, 0, tb * 512 + tap:tb * 512 + tap + 512]

            for tap in range(NPE, 31):
                for tb in range(2):
                    yk = (Yn, j * 2 + tb)
                    if tap == NPE:
                        TS("dve", yblk(tb), ushift(tb, tap), DWT[:, l, j, tap:tap + 1], None, ALU.mult, None, [(UPn, 0), C], [yk])
                    else:
                        STT("dve", yblk(tb), ushift(tb, tap), DWT[:, l, j, tap:tap + 1], yblk(tb), ALU.mult, ALU.add,
                            [(UPn, 0), C, yk], [yk])
            for tb in range(2):
                b = psa.get()
                if pi == 0:
                    for nn in range(2):
                        n = 2 * tb + nn
                        for tap in range(NPE):
                            MM(PSb(b)[:, nn * 256:(nn + 1) * 256], DGv[:, tap, :], UP[:, j, n, tap:tap + 256],
                               tap == 0, tap == NPE - 1, [(DGn, 0), (UPn, 0)], [("ps", b)])
                    pv_ = PSb(b).rearrange("p (n s) -> p n s", s=256)
                else:
                    for tap in range(NPE):
                        MM(PSb(b), DGv[:, tap, :], UP[:, j, 0, tb * 512 + tap:tb * 512 + tap + 512],
                           tap == 0, tap == NPE - 1, [(DGn, 0), (UPn, 0)], [("ps", b)])
                    pv_ = PSb(b)
                STT("dve", yblk(tb), pv_, DWBT[:, l, j:j + 1], yblk(tb), ALU.add, ALU.add,
                    [("ps", b), C, (Yn, j * 2 + tb)], [(Yn, j * 2 + tb)])

        YB, YBn = arena("YB", 0, 4 * KB, BF16, (4, 512))
        YS, YSn = arena("YS", 4 * KB, 4 * KB, BF16, (4, 512))
        MEAN, MEn = arena("MEAN", 8 * KB, 2 * KB, F32)
        MSQ, MSn = arena("MSQ", 10 * KB, 2 * KB, F32)
        RS, RSn = arena("RS", 12 * KB, 2 * KB, F32)
        TB_, TBn = arena("TB", 14 * KB, 8 * KB, F32, (4, 512))
        ZB, ZBn = arena("ZB", 41 * KB, 8 * KB, BF16, (4, 1024))
        def ln(tb):
            ysl = Y[:, :, tb * 512:(tb + 1) * 512]
            YK = [(Yn, j_ * 2 + tb) for j_ in range(4)]
            ACTV(YB, ysl, AF.Identity, YK, [(YBn, 0)])
            ACTV(YS, ysl, AF.Square, YK, [(YSn, 0)])
            bm = psa.get()
            for j in range(4):
                MM(PSb(bm), ONESM[:], YB[:, j, :], j == 0, j == 3, [C, (YBn, 0)], [("ps", bm)])
            be = psa.get()
            for j in range(4):
                MM(PSb(be), ONESM[:], YS[:, j, :], j == 0, j == 3, [C, (YSn, 0)], [("ps", be)])
            ACTV(MEAN, PSb(bm), AF.Identity, [("ps", bm)], [(MEn, 0)])
            ACTV(MSQ, PSb(bm), AF.Square, [("ps", bm)], [(MSn, 0)])
            STT("dve", RS, PSb(be), EPS, MSQ, ALU.add, ALU.subtract, [("ps", be), (MSn, 0)], [(RSn, 0)])
            ACTV(RS, RS, AF.Ln, [(RSn, 0)], [(RSn, 0)])
            ACTV(RS, RS, AF.Exp, [(RSn, 0)], [(RSn, 0)], scale=-0.5)
            TT("dve", TB_, ysl, MEAN.unsqueeze(1).broadcast_to([128, 4, 512]), ALU.subtract, YK + [(MEn, 0)], [(TBn, 0)])
            TT("dve", TB_, TB_, RS.unsqueeze(1).broadcast_to([128, 4, 512]), ALU.mult, [(TBn, 0), (RSn, 0)], [(TBn, 0)])
            for j in range(4):
                ACTV(ZB[:, j, tb * 512:(tb + 1) * 512], TB_[:, j, :], AF.Silu, [(TBn, 0), C], [(ZBn, tb)],
                     scale=LNGT[:, l, j:j + 1], bias=LNBT[:, l, j:j + 1])

        ln(0)
        def ev_cg(j, tb, b):
            ACTV(CAT[:, j, tb * 512:(tb + 1) * 512], PSb(b), AF.Silu, [("ps", b)], [("CAT", j)])

        proj_fm(l, 10, ev_cg, mid=lambda: ln(1))

        for jo in range(4):
            for tb in range(2):
                b = psa.get()
                for j in range(4):
                    MM(PSb(b), WPW[:, j, jo * 128:(jo + 1) * 128], ZB[:, j, tb * 512:(tb + 1) * 512], j == 0, j == 3,
                       [(WWn, 0), (ZBn, tb)], [("ps", b)])
                TT("dve", CAT[:, jo, tb * 512:(tb + 1) * 512], PSb(b), CAT[:, jo, tb * 512:(tb + 1) * 512], ALU.mult,
                   [("ps", b), ("CAT", jo)], [("CAT", jo)])
        wout(l, 3)

    def attention(pi, l):
        NS, S = (4, 256) if pi == 0 else (1, 1024)
        NKT = 2 if pi == 0 else 10
        KOFF = 0 if pi == 0 else 256
        SK = S + KOFF
        NVT = 8 if pi == 0 else 10
        vx_sz = (8 * KB + 512) if pi == 0 else (10 * KB + 512)
        kt_sz = 8 * KB if pi == 0 else 10 * KB
        a_kt = vx_sz
        a_qz = a_kt + kt_sz
        o2 = a_qz + 16 * KB
        VX, VXn = arena("VX", 0, vx_sz, BF16)
        KT, KTn = arena("KT", a_kt, kt_sz, BF16)
        QZ, QZn = arena("QZ", a_qz, 16 * KB, BF16)
        QZv = QZ.rearrange("p (h q m s) -> p h q m s", q=4, m=2, s=256)
        VXv = VX[:, 0:NVT * 4 * 130].rearrange("p (t h d) -> p t h d", h=4, d=130)
        KTv = KT[:, 0:4 * NS * SK].rearrange("p (h s) -> p h s", s=NS * SK)
        MEMSET("pool", VX, 1.0, [(VXn, "all")])
        MEMSET("pool", QZ, 0.0, [(QZn, "all")])

        if pi == 0:
            ST, STn = arena("ST", o2, 8 * KB, F32, (4, 512))
        stc = [0]

        def stage_out(dst, hf, tp, b):
            i = stc[0] % 4
            stc[0] += 1
            CP("act", ST[:, i, :], PSb(b), [("ps", b)], [(STn, i)])
            for tt in range(2):
                t = 2 * tp + tt
                n, st_ = t // 2, t % 2
                dd = D[dst][n, l, 2 * hf:2 * hf + 2, st_ * 128:(st_ + 1) * 128, :].rearrange("h s d -> s h d")
                DMA("sp", dd, ST[:, i, tt * 256:(tt + 1) * 256].rearrange("p (h d) -> p h d", d=128), [(STn, i)], [],
                    "ST%d" % i, is_output=True)
            return i

        vt0 = 0 if pi == 0 else 2

        def ev_v(hf, tp, b):
            dstv = VXv[:, vt0 + 2 * tp:vt0 + 2 * tp + 2, 2 * hf:2 * hf + 2, 0:128]
            wk = [(VXn, vt0 + 2 * tp), (VXn, vt0 + 2 * tp + 1)]
            if pi == 0:
                i = stage_out("nv", hf, tp, b)
                CP("dve", dstv, ST[:, i, :].rearrange("p (t h d) -> p t h d", h=2, d=128), [(STn, i), (VXn, "all")], wk)
            else:
                CP("dve", dstv, PSb(b).rearrange("p (t h d) -> p t h d", h=2, d=128), [("ps", b), (VXn, "all")], wk)

        proj_tm(l, 6, ev_v)
        stage("a2%d%d" % (pi, l))
        if pi == 0:
            KRAW, KRn = arena("KRAW", o2 + 8 * KB, 4 * KB, F32, (2, 512))
        else:
            ROPE, ROn = arena("ROPE", o2, 8 * KB, F32, (2, 1024))
            RAW, RWn = arena("RAW", o2 + 8 * KB, 4 * KB, F32, (2, 512))
            CKB, CKn = arena("CKB", o2 + 12 * KB, 2 * KB, BF16, (4, 2, 128))
            DMA("sp", ROPE, D["c_rope"], [], [(ROn, 0)], "ROPE")
            for kt_ in range(2):
                DMA("pool", VXv[:, kt_, :, 0:128], D["cv"][l][:, kt_ * 128:(kt_ + 1) * 128, :].rearrange("h p d -> p h d"),
                    [(VXn, "all")], [(VXn, kt_)], "CVL")
                DMA("pool", CKB[:, :, kt_, :], D["ck"][l][:, kt_ * 128:(kt_ + 1) * 128, :].rearrange("h p d -> p h d"),
                    [], [(CKn, 0)], "CKL")
            b = psa.get()
            pb = PSb(b).bitcast(BF16)
            for h in range(4):
                for kt in range(2):
                    TRN(pb[:, (h * 2 + kt) * 128:(h * 2 + kt + 1) * 128], CKB[:, h, kt, :], [(CKn, 0)], [("ps", b)])
            CP("dve", KTv[:, :, 0:256], pb.rearrange("p (h s) -> p h s", s=256), [("ps", b)], [(KTn, "ctx")])
            T1, T1n = arena("T1", o2 + 12 * KB, 2 * KB, F32)
            T2, T2n = arena("T2", o2 + 14 * KB, 2 * KB, F32)
        stage("a3%d%d" % (pi, l))
        rc = [0]

        def roped(j, tb, b):
            i = rc[0] % 2
            rc[0] += 1
            CP("act", RAW[:, i, :], PSb(b), [("ps", b)], [(RWn, i)])
            b2 = psa.get()
            MM(PSb(b2), RPERM[:], RAW[:, i, :], True, True, [C, (RWn, i)], [("ps", b2)])
            TT("pool", T1, RAW[:, i, :], ROPE[:, 0, tb * 512:(tb + 1) * 512], ALU.mult, [(RWn, i), (ROn, 0)], [(T1n, 0)])
            TT("dve", T2, PSb(b2), ROPE[:, 1, tb * 512:(tb + 1) * 512], ALU.mult, [("ps", b2), (ROn, 0)], [(T2n, 0)])

        def ev_k(j, tb, b):
            dst = KTv[:, j, KOFF + tb * 512:KOFF + (tb + 1) * 512]
            if pi == 0:
                CP("act", dst, PSb(b), [("ps", b)], [(KTn, j * 2 + tb)])
                i = rc[0] % 2
                rc[0] += 1
                CP("act", KRAW[:, i, :], PSb(b), [("ps", b)], [(KRn, i)])
                b2 = psa.get()
                for q in range(4):
                    P.op("pe", lambda e, q=q: e.transpose(out=PSb(b2)[:, q * 128:(q + 1) * 128],
                                                          in_=KRAW[:, i, q * 128:(q + 1) * 128], identity=IDENTF[:]),
                         [(KRn, i), C], [("ps", b2)])
                si = stc[0] % 4
                stc[0] += 1
                CP("act", ST[:, si, :], PSb(b2), [("ps", b2)], [(STn, si)])
                for nn in range(2):
                    dd = D["nk"][2 * tb + nn, l, j, :, :].rearrange("(q p) d -> p q d", p=128)
                    DMA("sp", dd, ST[:, si, nn * 256:(nn + 1) * 256].rearrange("p (q d) -> p q d", d=128), [(STn, si)], [],
                        "ST%d" % si, is_output=True)
            else:
                roped(j, tb, b)
                TT("dve", dst, T1, T2, ALU.add, [(T1n, 0), (T2n, 0)], [(KTn, j * 2 + tb)])

        def ev_q(j, tb, b):
            for m in range(2):
                rows = slice(m * 64, (m + 1) * 64)
                dst = QZv[rows, j, 2 * tb:2 * tb + 2, m, :]
                if pi == 0:
                    CP("act", dst, PSb(b)[rows, :].rearrange("p (q s) -> p q s", s=256), [("ps", b), (QZn, "all")],
                       [(QZn, j * 2 + tb)])
                else:
                    if m == 0:
                        roped(j, tb, b)
                    TT("dve", dst, T1[rows, :].rearrange("p (q s) -> p q s", s=256),
                       T2[rows, :].rearrange("p (q s) -> p q s", s=256), ALU.add,
                       [(T1n, 0), (T2n, 0), (QZn, "all")], [(QZn, j * 2 + tb)])

        proj_fm(l, 5, ev_k)
        stage("a4%d%d" % (pi, l))
        proj_fm(l, 4, ev_q)

        AG, AGn = arena("AG", o2, 8 * KB, BF16, (8, 512))

        def ev_ag(hf, tp, b):
            agv = AG[:, 2 * tp:2 * tp + 2, hf * 256:(hf + 1) * 256]
            ACTV(agv, PSb(b).rearrange("p (t n) -> p t n", n=256), AF.Silu,
                 [("ps", b)], [(AGn, 2 * tp), (AGn, 2 * tp + 1)])
            for tt in range(2):
                a1 = AG[:, 2 * tp + tt, hf * 256:(hf + 1) * 256].rearrange("p (h d) -> p h d", d=128)
                TT("pool", a1, a1, SGBC[:, l, :].unsqueeze(1).broadcast_to([128, 2, 128]), ALU.mult,
                   [(AGn, 2 * tp + tt), ("SGBC", 0)], [(AGn, 2 * tp + tt)])

        proj_tm(l, 7, ev_ag)
        stage("core%d%d" % (pi, l))

        EB, EBn = arena("EB", o2 + 8 * KB, 3 * KB, BF16, (3, 512))
        fo = o2 + 11 * KB
        QN = 256
        NQT = 2
        NSET = 3
        sets = []
        for si in range(NSET):
            f0 = fo + si * 1664
            sets.append(dict(
                R=arena("R%d" % si, f0, 32, F32, (2, 2)), R1=arena("R1%d" % si, f0 + 32, 32, F32),
                SS=arena("SS%d" % si, f0 + 64, 32, F32), RS=arena("RSa%d" % si, f0 + 96, 32, F32),
                O0=arena("O0%d" % si, f0 + 128, 1 * KB, F32, (2, 128)),
                OA=arena("OA%d" % si, f0 + 128 + KB, 512, BF16, (2, 128))))
        groups = []
        for n in range(NS):
            for h in range(4):
                for qb in range(S // QN):
                    groups.append((n, h, qb))
        steps = []
        for gi in range(len(groups)):
            for kt in range(NKT):
                steps.append((gi, kt))
        DEPTH = 2

        def ginfo(gi):
            n, h, qb = groups[gi]
            par = gi % NSET
            ob = 2 + 2 * par
            Ov = PS[:, ob * 512:(ob + 2) * 512].rearrange("p (q c) -> p q c", c=512)
            return n, h, qb, par, ob, Ov, n * S + qb * QN

        def score(si):
            gi, kt = steps[si]
            n, h, qb, par, ob, Ov, q0 = ginfo(gi)
            bs = si % 2
            k0 = n * SK + kt * 128
            qbg = q0 // 256
            MM(PSb(bs), KTv[:, h, k0:k0 + 128], QZv[:, h, qbg, :, :], True, True,
               [(KTn, x) for x in ("ctx", h * 2, h * 2 + 1)] + [(QZn, "all"), (QZn, h * 2 + qbg // 2)], [("ps", bs)])
            ei = si % 3
            ACTV(EB[:, ei, :], PSb(bs), AF.Exp, [("ps", bs)], [(EBn, ei)], scale=0.125)

        def pv(si):
            gi, kt = steps[si]
            n, h, qb, par, ob, Ov, q0 = ginfo(gi)
            ei = si % 3
            vt = (n * 2 + kt) if pi == 0 else kt
            for m in range(2):
                for qt in range(NQT):
                    P.op("pe", lambda e, m=m, qt=qt: e.matmul(
                        out=Ov[:, qt, m * 130:(m + 1) * 130], lhsT=EB[:, ei, m * 256 + qt * 128:m * 256 + (qt + 1) * 128],
                        rhs=VXv[:, vt, h, :], start=(kt == 0 and m == 0), stop=(kt == NKT - 1), skip_group_check=True),
                        [(EBn, ei), (VXn, vt), (VXn, "all")], [("ps", ob + qt)])
            if kt == NKT - 1:
                finalize(gi)
                if gi >= 1:
                    finalize_b(gi - 1)
                if gi >= 2:
                    finalize_pe(gi - 2)
                if gi == len(groups) - 1:
                    finalize_b(gi)
                    finalize_pe(gi - 1)
                    finalize_pe(gi)

        def finalize(gi):
            n, h, qb, par, ob, Ov, q0 = ginfo(gi)
            st_ = sets[par]
            (R_, Rn), (R1, R1n), (SSa, SSn), (RSa, RSan) = st_["R"], st_["R1"], st_["SS"], st_["RS"]
            R_ = R_[:, 0:2, :]
            R1 = R1[:, 0:NQT]
            (O0, O0n), (OA, OAn) = st_["O0"], st_["OA"]
            OR = [("ps", ob + qt) for qt in range(NQT)]
            Ow = Ov[:, :, 0:260].rearrange("p q (m d) -> p q m d", d=130)
            P.op("dve", lambda e: e.reciprocal(out=R_, in_=Ow[:, :, :, 128]), OR, [(Rn, 0)])
            TS("dve", R1, R_[:, :, 1], NLAM[:, l:l + 1], None, ALU.mult, None, [(Rn, 0), ("NLAM", 0)], [(R1n, 0)])
            TT("dve", O0, Ow[:, :, 0, 0:128], R_[:, :, 0:1].broadcast_to([128, NQT, 128]), ALU.mult, OR + [(Rn, 0)], [(O0n, 0)])
            for qt in range(NQT):
                STT("dve", O0[:, qt, :], Ow[:, qt, 1, 0:128], R1[:, qt:qt + 1], O0[:, qt, :], ALU.mult, ALU.add,
                    OR + [(R1n, 0), (O0n, 0)], [(O0n, 0)])
            for qt in range(NQT):
                P.op("dve", lambda e, qt=qt: e.scalar_tensor_tensor(out=OA[:, qt, :], in0=O0[:, qt, :], scalar=1.0, in1=O0[:, qt, :],
                                                                    op0=ALU.mult, op1=ALU.mult, accum_out=SSa[:, qt:qt + 1]),
                     [(O0n, 0)], [(OAn, 0), (SSn, qt)])
            TS("pool", RSa[:, 0:NQT], SSa[:, 0:NQT], 1.0 / 128.0, EPS, ALU.mult, ALU.add, [(SSn, 0), (SSn, 1)], [(RSan, 0)])
            TT("pool", RSa[:, 0:NQT], RSa[:, 0:NQT], MHALF[:, 0:NQT], ALU.pow, [(RSan, 0), C], [(RSan, 0)])

        def finalize_b(gi):
            n, h, qb, par, ob, Ov, q0 = ginfo(gi)
            st_ = sets[par]
            (RSa, RSan), (O0, O0n), (OA, OAn) = st_["RS"], st_["O0"], st_["OA"]
            tl0 = q0 // 128
            for qt in range(NQT):
                STT("dve", OA[:, qt, :], O0[:, qt, :], RSa[:, qt:qt + 1], AG[:, tl0 + qt, h * 128:(h + 1) * 128], ALU.mult, ALU.mult,
                    [(O0n, 0), (RSan, 0), (AGn, tl0 + qt)], [(OAn, 0)])

        def finalize_pe(gi):
            n, h, qb, par, ob, Ov, q0 = ginfo(gi)
            (OA, OAn) = sets[par]["OA"]
            pbt = PSb(ob).bitcast(BF16)[:, 640:896]
            for qt in range(NQT):
                TRN(pbt[:, qt * 128:(qt + 1) * 128], OA[:, qt, :], [(OAn, 0)], [("ps", ob)])
            CP("act", CAT[:, h, q0:q0 + QN], pbt, [("ps", ob)], [("CAT", h)])

        for si in range(len(steps) + DEPTH):
            if si < len(steps):
                score(si)
            if si - DEPTH >= 0:
                pv(si - DEPTH)
        wout(l, 2)

    def final_norm(pi):
        yout = D["yp"] if pi == 0 else D["ys"]
        FGB, FGn = arena("FGB", 0, 8 * KB, F32)
        JUNK, Jn = arena("JUNK", 8 * KB, 4 * KB, BF16)
        YO, YOn = arena("YO", 12 * KB, 16 * KB, F32, (2, 2048))
        SSF, SSFn = arena("SSF", 28 * KB, 32, F32)
        TMF, TMFn = arena("TMF", 28 * KB + 32, 32, F32)
        RSF, RSFn = arena("RSF", 28 * KB + 64, 32, F32)
        DMA("sp", FGB, D["final_g"].partition_broadcast(128), [], [(FGn, 0)], "FGB")
        for hf in range(2):
            ts_ = range(4 * hf, 4 * hf + 4)
            for t in ts_:
                ACTV(JUNK, X[:, t, :], AF.Square, XK(t), [(Jn, 0), (SSFn, t)], accum_out=SSF[:, t:t + 1])
            sl = slice(4 * hf, 4 * hf + 4)
            TS("pool", TMF[:, sl], SSF[:, sl], 1.0 / 2048.0, EPS, ALU.mult, ALU.add, [(SSFn, t) for t in ts_], [(TMFn, hf)])
            TT("pool", RSF[:, sl], TMF[:, sl], MHALF[:, 0:4], ALU.pow, [(TMFn, hf), C], [(RSFn, hf)])
            for t in ts_:
                i = t % 2
                STT("dve", YO[:, i, :], X[:, t, :], RSF[:, t:t + 1], FGB, ALU.mult, ALU.mult, XK(t) + [(RSFn, hf), (FGn, 0)], [(YOn, i)])
                DMA("sp", yout[t * 128:(t + 1) * 128, :], YO[:, i, :], [(YOn, i)], [], "YO%d" % i, is_output=True)
                if pi == 0:
                    DMA("sp", X[:, t, :], D["xs"][t * 128:(t + 1) * 128, :], [], XK(t), "X%d" % t)

    for pi in range(2):
        if pi == 0:
            for t in range(8):
                DMA("sp", X[:, t, :], D["xp"][t * 128:(t + 1) * 128, :], [], XK(t), "X%d" % t)
        for l in range(2):
            stage("norm%d%d" % (pi, l))
            norm_phase(l, pi)
            stage("fourier%d%d" % (pi, l))
            fourier(pi, l)
            stage("pool%d%d" % (pi, l))
            poolmix(pi, l)
            stage("conv%d%d" % (pi, l))
            convmod(pi, l)
            stage("attn%d%d" % (pi, l))
            flush_mod()
            attention(pi, l)
        stage("final%d" % pi)
        final_norm(pi)


def build_nc():
    nc = bass.Bass("TRN2", target_bir_lowering=False)
    es = ExitStack()
    D = {}
    for k, shp in IN_SPECS.items():
        D[k] = nc.dram_tensor(k, list(shp), F32, kind="ExternalInput").ap()
    for k, (shp, dt) in CONST_SPECS.items():
        D[k] = nc.dram_tensor(k, list(shp), dt, kind="ExternalInput").ap()
    for k, shp in OUT_SPECS.items():
        D[k] = nc.dram_tensor(k, list(shp), F32, kind="ExternalOutput").ap()
    T = {"dram": D}

    def sb(name, shape, dt):
        T[name] = es.enter_context(nc.sbuf_tensor(name, list(shape), dt))

    sb("X", [128, 8, 2048], F32)
    sb("HT", [128, 16, 1024], BF16)
    sb("CAT", [128, 4, 1024], BF16)
    sb("RING", [128, 3, 4096], BF16)
    sb("GBC", [128, 2048], F32)
    sb("ARENA", [128, ARENA_BYTES // 2], BF16)
    sb("WTMP", [128, 2, 512], F32)
    sb("IDENT", [128, 128], BF16)
    sb("IDENTF", [128, 128], F32)
    sb("RPERM", [128, 128], F32)
    sb("ONESM", [128, 128], BF16)
    sb("CCSC", [128, 2, 128], BF16)
    sb("BAND", [128, 2560], BF16)
    sb("DFT256", [128, 2, 2, 256], BF16)
    sb("MHALF", [128, 8], F32)
    sb("SCT", [128, 16, 2], BF16)
    sb("CVT", [128, 2, 16], F32)
    sb("BMT", [128, 2, 48], F32)
    sb("GT_", [128, 2, 16], F32)
    sb("MODT", [128, 2, 48, 2], F32)
    sb("AT", [128, 2, 16, 2], F32)
    sb("TMPA", [128, 2, 16, 2], F32)
    for n in ("PST", "DWBT", "LNGT", "LNBT"):
        sb(n, [128, 2, 4], F32)
    sb("DWT", [128, 2, 4, 31], F32)
    sb("SGBC", [128, 2, 128], F32)
    sb("NLAM", [128, 2], F32)
    sb("EL", [128, 4], F32)
    sb("SL", [128, 4], F32)
    sb("NL0", [128, 2], F32)
    sb("SSQ", [128, 8], F32)
    sb("RSTD", [128, 8], F32)
    sb("TMP8", [128, 8], F32)
    T["PS"] = es.enter_context(nc.psum_tensor("PS", [128, 4096], F32))

    Pd = Prog(nc, es, dry=True)
    rd = Ring(Pd, T["RING"], plan=None)
    build_all(nc, es, T, Pd, rd)
    P = Prog(nc, es)
    ring = Ring(P, T["RING"], plan=rd.rec)
    build_all(nc, es, T, P, ring)
    assert ring.consumed == len(rd.rec)
    if STOP is not None:
        pass
    P.finalize()
    P.emit()
    return nc, es, P


_CACHE = {}


def kernel(x_prompt, x_sample, cache_k, cache_v, c, c_ctx, norm_g, w_mod, b_mod, w_in, w_fourier, w_pool, pool_scale,
           diff_lambda, subln_g, conv_dw, conv_dw_b, conv_ln_g, conv_ln_b, w_conv_pw, w_out, final_g):
    f = lambda a: np.ascontiguousarray(np.asarray(a, dtype=np.float32))
    if "nc" not in _CACHE:
        _CACHE["nc"] = build_nc()
        _CACHE["consts"] = make_consts()
    nc, es, P = _CACHE["nc"]
    consts = _CACHE["consts"]
    x_prompt, x_sample, cache_k, cache_v, c, c_ctx = map(f, (x_prompt, x_sample, cache_k, cache_v, c, c_ctx))
    shared = {
        "norm_g": f(norm_g), "w_mod": f(w_mod), "b_mod": f(b_mod), "w_in": f(w_in), "w_fourier": f(w_fourier),
        "w_pool": f(w_pool), "pool_scale": f(pool_scale), "diff_lambda": f(diff_lambda), "subln_g": f(subln_g),
        "conv_dw": f(conv_dw), "conv_dw_b": f(conv_dw_b), "conv_ln_g": f(conv_ln_g), "conv_ln_b": f(conv_ln_b),
        "w_conv_pw": f(w_conv_pw), "w_out": f(w_out), "final_g": f(final_g).reshape(1, 2048),
    }
    shared.update(consts)
    in_maps = []
    for i in range(8):
        m = dict(shared)
        m["xp"] = x_prompt[4 * i:4 * i + 4].reshape(1024, 2048)
        m["xs"] = x_sample[i]
        m["ck"] = cache_k[i]
        m["cv"] = cache_v[i]
        m["cvec"] = np.ascontiguousarray(np.stack([c_ctx, c[i]]))
        in_maps.append(m)
    res = run_bass_kernel_spmd(nc, in_maps, core_ids=list(range(8)))
    rs = res.results
    y_prompt = np.concatenate([r["yp"].reshape(4, 256, 2048) for r in rs], axis=0)
    y_sample = np.stack([r["ys"] for r in rs], axis=0)
    nk = np.concatenate([r["nk"] for r in rs], axis=0)
    nv = np.concatenate([r["nv"] for r in rs], axis=0)
    return (y_prompt.astype(np.float32), y_sample.astype(np.float32), nk.astype(np.float32), nv.astype(np.float32))
```
